# Optimizing a Trainium2 kernel written in Bass

```python
import math
import jax
import jax.numpy as jnp
from jax import lax
import numpy as np

D_MODEL = 1024
BATCH = 32
SEQ = 256
DEPTH = 4
DEC_BATCH = 8
DEC_SEQ = 4096
PAST_LEN = 256

GRID_W = 64
N_MIXERS = 3
N_GDN = (DEPTH + 2) // 3
N_DIFF = (DEPTH + 1) // 3
N_S5 = DEPTH // 3
GDN_HEADS = 8
GDN_DK = 128
GDN_DV = 128
GDN_CONV = 3
GDN_CHUNK = 64
DIFF_DH = 64
DIFF_HEADS = D_MODEL // (2 * DIFF_DH)
Q_BLOCK = 128
ROPE_BASE = 10000.0
S5_GROUP_CH = 16
S5_GROUPS = D_MODEL // S5_GROUP_CH
S5_STATE = 64
FFN_DIM = 2816
FFN_CONV = 3
MOD_CHUNKS = 6
EPS = 1e-6

kernel_name = 'hybrid_diffusion_gdn_diffattn_s5_step'


def rms_norm(x, gain):
    xf = x.astype(jnp.float32)
    y = xf * lax.rsqrt(jnp.mean(xf * xf, axis=-1, keepdims=True) + EPS)
    return (y * gain.astype(jnp.float32)).astype(x.dtype)


def l2_normalize(x):
    return x * lax.rsqrt(jnp.sum(x * x, axis=-1, keepdims=True) + EPS)


def depthwise_conv_centred(x, w):
    k, ch = w.shape
    return lax.conv_general_dilated(
        x, w[:, None, :].astype(x.dtype), window_strides=(1,), padding=[(k // 2, k // 2)],
        dimension_numbers=('NWC', 'WIO', 'NWC'), feature_group_count=ch)


def axial_rope_tables(n_tokens, dim):
    n_rows = n_tokens // GRID_W
    rows = jnp.repeat(jnp.arange(n_rows, dtype=jnp.float32), GRID_W)
    cols = jnp.tile(jnp.arange(GRID_W, dtype=jnp.float32), n_rows)
    n_freq = dim // 4
    inv_freq = ROPE_BASE ** (-jnp.arange(n_freq, dtype=jnp.float32) / n_freq)
    ang_r = rows[:, None] * inv_freq
    ang_c = cols[:, None] * inv_freq
    ang = jnp.concatenate([ang_r, ang_r, ang_c, ang_c], axis=-1)
    return jnp.cos(ang), jnp.sin(ang)


def rotate_half(z):
    z1, z2 = jnp.split(z, 2, axis=-1)
    return jnp.concatenate([-z2, z1], axis=-1)


def apply_axial_rope(x, cos, sin):
    xr, xc = jnp.split(x, 2, axis=-1)
    x_rot = jnp.concatenate([rotate_half(xr), rotate_half(xc)], axis=-1)
    cos = cos[None, :, None, None, :].astype(x.dtype)
    sin = sin[None, :, None, None, :].astype(x.dtype)
    return x * cos + x_rot * sin


def gdn_chunked(q, k, v, g, beta, s0):
    b, seq, h, dk = q.shape
    dv = v.shape[-1]
    c = GDN_CHUNK
    n = seq // c

    def to_chunks(t):
        t = t.reshape((b, n, c, h) + t.shape[3:])
        return jnp.moveaxis(t, (1, 3), (0, 2))

    q = to_chunks(q) * (dk ** -0.5)
    k = to_chunks(k)
    v = to_chunks(v)
    g = to_chunks(g)
    beta = to_chunks(beta)
    gc = jnp.cumsum(g, axis=-1)
    causal = jnp.tril(jnp.ones((c, c), dtype=bool))
    strict = jnp.tril(jnp.ones((c, c), dtype=bool), -1)
    gdiff = gc[..., :, None] - gc[..., None, :]
    decay = jnp.where(causal, jnp.exp(jnp.where(causal, gdiff, 0.0)), 0.0)
    kb = k * beta[..., None]
    lower = jnp.where(strict, jnp.einsum('nbhid,nbhjd->nbhij', kb, k) * decay, 0.0)
    rhs = jnp.concatenate([v * beta[..., None], kb * jnp.exp(gc)[..., None]], axis=-1)
    sol = lax.linalg.triangular_solve(lower + jnp.eye(c, dtype=q.dtype), rhs, left_side=True,
                                      lower=True, unit_diagonal=True)
    u, w = sol[..., :dv], sol[..., dv:]
    intra = jnp.where(causal, jnp.einsum('nbhid,nbhjd->nbhij', q, k) * decay, 0.0)
    q_dec = q * jnp.exp(gc)[..., None]
    k_dec = k * jnp.exp(gc[..., -1:] - gc)[..., None]
    g_last = jnp.exp(gc[..., -1])

    def step(s, xs):
        q_i, k_i, u_i, w_i, a_i, gl_i = xs
        v_new = u_i - jnp.einsum('bhcd,bhde->bhce', w_i, s)
        o_i = jnp.einsum('bhcd,bhde->bhce', q_i, s) + jnp.einsum('bhij,bhje->bhie', a_i, v_new)
        s = s * gl_i[..., None, None] + jnp.einsum('bhcd,bhce->bhde', k_i, v_new)
        return s, o_i

    s_fin, o = lax.scan(step, s0, (q_dec, k_dec, u, w, intra, g_last))
    o = jnp.moveaxis(o, (0, 2), (1, 3)).reshape(b, seq, h, dv)
    return o, s_fin


def gdn_mixer(h, w_qkv, conv_w, w_gate, w_alpha, w_beta, a_log, dt_bias, norm_g, w_out, cache):
    b, seq, _ = h.shape
    hk = GDN_HEADS * GDN_DK
    qkv = jax.nn.silu(depthwise_conv_centred(h @ w_qkv, conv_w)).astype(jnp.float32)
    q, k, v = jnp.split(qkv, [hk, 2 * hk], axis=-1)
    q = l2_normalize(q.reshape(b, seq, GDN_HEADS, GDN_DK))
    k = l2_normalize(k.reshape(b, seq, GDN_HEADS, GDN_DK))
    v = v.reshape(b, seq, GDN_HEADS, GDN_DV)
    alpha = (h @ w_alpha).astype(jnp.float32).reshape(b, seq, 2, GDN_HEADS)
    beta = jax.nn.sigmoid((h @ w_beta).astype(jnp.float32)).reshape(b, seq, 2, GDN_HEADS)
    g = -jnp.exp(a_log.astype(jnp.float32)) * jax.nn.softplus(alpha + dt_bias.astype(jnp.float32))
    if cache is None:
        s0 = jnp.zeros((b, 2, GDN_HEADS, GDN_DK, GDN_DV), jnp.float32)
    else:
        s0 = cache[0].astype(jnp.float32)
    o_f, s_f = gdn_chunked(q, k, v, g[:, :, 0], beta[:, :, 0], s0[:, 0])
    flip = lambda t: jnp.flip(t, axis=1)
    o_b, s_b = gdn_chunked(flip(q), flip(k), flip(v), flip(g[:, :, 1]), flip(beta[:, :, 1]), s0[:, 1])
    o = o_f + flip(o_b)
    gate = jax.nn.silu((h @ w_gate).astype(jnp.float32)).reshape(b, seq, GDN_HEADS, GDN_DV)
    o = rms_norm(o, norm_g) * gate
    y = o.reshape(b, seq, GDN_HEADS * GDN_DV).astype(h.dtype) @ w_out
    return y, (jnp.stack([s_f, s_b], axis=1),)


def diff_softmax_attend(q, k, v, lam):
    b, lq, h, _, dh = q.shape
    nb = lq // Q_BLOCK
    qb = jnp.moveaxis(q.reshape(b, nb, Q_BLOCK, h, 2, dh), 1, 0)
    scale = dh ** -0.5

    def one_block(qblk):
        s = jnp.einsum('bqhcd,bkhcd->bhcqk', qblk, k).astype(jnp.float32) * scale
        p = jax.nn.softmax(s, axis=-1)
        pd = p[:, :, 0] - lam * p[:, :, 1]
        return jnp.einsum('bhqk,bkhe->bqhe', pd.astype(v.dtype), v)

    o = lax.map(one_block, qb)
    return jnp.moveaxis(o, 0, 1).reshape(b, lq, h, 2 * dh)


def diff_attention_mixer(h, w_qkv, lam_vecs, subln, w_out, lam_init, cache):
    b, seq, _ = h.shape
    q, k, v = jnp.split(h @ w_qkv, 3, axis=-1)
    q = q.reshape(b, seq, DIFF_HEADS, 2, DIFF_DH)
    k = k.reshape(b, seq, DIFF_HEADS, 2, DIFF_DH)
    v = v.reshape(b, seq, DIFF_HEADS, 2 * DIFF_DH)
    if cache is None:
        k_all, v_all = k, v
    else:
        cos, sin = axial_rope_tables(seq, DIFF_DH)
        q = apply_axial_rope(q, cos, sin)
        k = apply_axial_rope(k, cos, sin)
        k_all = jnp.concatenate([cache[0].astype(k.dtype), k], axis=1)
        v_all = jnp.concatenate([cache[1].astype(v.dtype), v], axis=1)
    lf = lam_vecs.astype(jnp.float32)
    lam = jnp.exp(jnp.sum(lf[0] * lf[1])) - jnp.exp(jnp.sum(lf[2] * lf[3])) + lam_init
    o = diff_softmax_attend(q, k_all, v_all, lam)
    o = rms_norm(o, subln) * (1.0 - lam_init)
    return o.reshape(b, seq, D_MODEL) @ w_out, (k, v)


def complex_affine_combine(e1, e2):
    a1r, a1i, b1r, b1i = e1
    a2r, a2i, b2r, b2i = e2
    return (a2r * a1r - a2i * a1i, a2r * a1i + a2i * a1r,
            a2r * b1r - a2i * b1i + b2r, a2r * b1i + a2i * b1r + b2i)


def s5_scan(u, a_re, a_im, log_dt, b_re, b_im, c_re, c_im, h0_re, h0_im, reverse):
    seq = u.shape[1]
    dt = jnp.exp(log_dt)[:, None]
    mag = jnp.exp(dt * a_re)
    abar_re = mag * jnp.cos(dt * a_im)
    abar_im = mag * jnp.sin(dt * a_im)
    den = a_re * a_re + a_im * a_im
    f_re = ((abar_re - 1.0) * a_re + abar_im * a_im) / den
    f_im = (abar_im * a_re - (abar_re - 1.0) * a_im) / den
    bbar_re = f_re[..., None] * b_re - f_im[..., None] * b_im
    bbar_im = f_re[..., None] * b_im + f_im[..., None] * b_re
    if reverse:
        u = jnp.flip(u, axis=1)
    bu_re = jnp.einsum('blgc,gpc->blgp', u, bbar_re)
    bu_im = jnp.einsum('blgc,gpc->blgp', u, bbar_im)
    bu_re = bu_re.at[:, 0].add(abar_re * h0_re - abar_im * h0_im)
    bu_im = bu_im.at[:, 0].add(abar_re * h0_im + abar_im * h0_re)
    a_seq_re = jnp.broadcast_to(abar_re, (1, seq) + abar_re.shape)
    a_seq_im = jnp.broadcast_to(abar_im, (1, seq) + abar_im.shape)
    _, _, x_re, x_im = lax.associative_scan(complex_affine_combine,
                                            (a_seq_re, a_seq_im, bu_re, bu_im), axis=1)
    y = jnp.einsum('gcp,blgp->blgc', c_re, x_re) - jnp.einsum('gcp,blgp->blgc', c_im, x_im)
    if reverse:
        y = jnp.flip(y, axis=1)
    return y, x_re[:, -1], x_im[:, -1]


def s5_mixer(h, a_re, a_im, log_dt, b_re, b_im, c_re, c_im, d_skip, w_glu, cache):
    b, seq, _ = h.shape
    f = lambda t: t.astype(jnp.float32)
    hf = f(h)
    u = hf.reshape(b, seq, S5_GROUPS, S5_GROUP_CH)
    if cache is None:
        h0_re = jnp.zeros((b, 2, S5_GROUPS, S5_STATE), jnp.float32)
        h0_im = jnp.zeros((b, 2, S5_GROUPS, S5_STATE), jnp.float32)
    else:
        h0_re, h0_im = f(cache[0]), f(cache[1])
    y = f(d_skip) * hf
    fin_re, fin_im = [], []
    for d in range(2):
        yd, fr, fi = s5_scan(u, f(a_re[d]), f(a_im[d]), f(log_dt[d]), f(b_re[d]), f(b_im[d]),
                             f(c_re[d]), f(c_im[d]), h0_re[:, d], h0_im[:, d], d == 1)
        y = y + yd.reshape(b, seq, D_MODEL)
        fin_re.append(fr)
        fin_im.append(fi)
    z = jax.nn.gelu(y).astype(h.dtype) @ w_glu
    val, gate = jnp.split(z, 2, axis=-1)
    return val * jax.nn.sigmoid(gate), (jnp.stack(fin_re, axis=1), jnp.stack(fin_im, axis=1))


def conv_ffn(h, w_up, conv_w, w_down):
    u = depthwise_conv_centred(h @ w_up, conv_w)
    gate, val = jnp.split(u, 2, axis=-1)
    return (jax.nn.silu(gate) * val) @ w_down


def trunk_layer(p, l, x, cond, cache):
    kind, j = l % N_MIXERS, l // N_MIXERS
    mod = (jax.nn.silu(cond) @ p['w_mod'][l] + p['b_mod'][l])[:, None, :]
    shift1, scale1, gate1, shift2, scale2, gate2 = jnp.split(mod, MOD_CHUNKS, axis=-1)
    gains = p['norm_gain'][l]
    h = rms_norm(x, gains[0]) * (1.0 + scale1) + shift1
    if kind == 0:
        y, state = gdn_mixer(h, p['w_gdn_qkv'][j], p['gdn_conv'][j], p['w_gdn_gate'][j],
                             p['w_gdn_alpha'][j], p['w_gdn_beta'][j], p['gdn_a_log'][j],
                             p['gdn_dt_bias'][j], p['gdn_norm'][j], p['w_gdn_out'][j], cache)
    elif kind == 1:
        lam_init = 0.8 - 0.6 * math.exp(-0.3 * l)
        y, state = diff_attention_mixer(h, p['w_diff_qkv'][j], p['diff_lam'][j], p['diff_subln'][j],
                                        p['w_diff_out'][j], lam_init, cache)
    else:
        y, state = s5_mixer(h, p['s5_a_re'][j], p['s5_a_im'][j], p['s5_log_dt'][j], p['s5_b_re'][j],
                            p['s5_b_im'][j], p['s5_c_re'][j], p['s5_c_im'][j], p['s5_d'][j],
                            p['w_s5_glu'][j], cache)
    x = x + gate1 * rms_norm(y, gains[1])
    h = rms_norm(x, gains[2]) * (1.0 + scale2) + shift2
    x = x + gate2 * rms_norm(conv_ffn(h, p['w_ffn_up'][l], p['ffn_conv'][l], p['w_ffn_down'][l]), gains[3])
    return x, state


def setup_inputs(seed: int = 0) -> dict:
    key = jax.random.key(seed)
    keys = iter(jax.random.split(key, 48))
    f32 = jnp.float32

    def nrm(shape, scale):
        return jax.random.normal(next(keys), shape, f32) * scale

    def unif(shape, lo, hi):
        return jax.random.uniform(next(keys), shape, f32, lo, hi)

    hk = GDN_HEADS * GDN_DK
    hv = GDN_HEADS * GDN_DV
    dt_gdn = jnp.exp(unif((N_GDN, 2, GDN_HEADS), math.log(1e-3), math.log(1e-1)))
    s5_n = jnp.pi * jnp.arange(S5_STATE, dtype=f32)
    return {
        'x_prompt': nrm((BATCH, SEQ, D_MODEL), 1.0),
        'x_sample': nrm((DEC_BATCH, DEC_SEQ, D_MODEL), 1.0),
        'state_l0_gdn': nrm((DEC_BATCH, 2, GDN_HEADS, GDN_DK, GDN_DV), 0.1),
        'cache_l1_k': nrm((DEC_BATCH, PAST_LEN, DIFF_HEADS, 2, DIFF_DH), 1.0),
        'cache_l1_v': nrm((DEC_BATCH, PAST_LEN, DIFF_HEADS, 2 * DIFF_DH), 1.0),
        'state_l2_s5_re': nrm((DEC_BATCH, 2, S5_GROUPS, S5_STATE), 0.5),
        'state_l2_s5_im': nrm((DEC_BATCH, 2, S5_GROUPS, S5_STATE), 0.5),
        'state_l3_gdn': nrm((DEC_BATCH, 2, GDN_HEADS, GDN_DK, GDN_DV), 0.1),
        'c': nrm((DEC_BATCH, D_MODEL), 1.0),
        'c_ctx': nrm((D_MODEL,), 1.0),
        'w_mod': nrm((DEPTH, D_MODEL, MOD_CHUNKS * D_MODEL), D_MODEL ** -0.5),
        'b_mod': nrm((DEPTH, MOD_CHUNKS * D_MODEL), 0.01),
        'norm_gain': 1.0 + nrm((DEPTH, 4, D_MODEL), 0.02),
        'w_ffn_up': nrm((DEPTH, D_MODEL, 2 * FFN_DIM), D_MODEL ** -0.5),
        'ffn_conv': nrm((DEPTH, FFN_CONV, 2 * FFN_DIM), FFN_CONV ** -0.5),
        'w_ffn_down': nrm((DEPTH, FFN_DIM, D_MODEL), FFN_DIM ** -0.5),
        'w_gdn_qkv': nrm((N_GDN, D_MODEL, 2 * hk + hv), D_MODEL ** -0.5),
        'gdn_conv': nrm((N_GDN, GDN_CONV, 2 * hk + hv), GDN_CONV ** -0.5),
        'w_gdn_gate': nrm((N_GDN, D_MODEL, hv), D_MODEL ** -0.5),
        'w_gdn_alpha': nrm((N_GDN, D_MODEL, 2 * GDN_HEADS), D_MODEL ** -0.5),
        'w_gdn_beta': nrm((N_GDN, D_MODEL, 2 * GDN_HEADS), D_MODEL ** -0.5),
        'gdn_a_log': jnp.log(unif((N_GDN, 2, GDN_HEADS), 1.0, 16.0)),
        'gdn_dt_bias': dt_gdn + jnp.log(-jnp.expm1(-dt_gdn)),
        'gdn_norm': 1.0 + nrm((N_GDN, GDN_DV), 0.02),
        'w_gdn_out': nrm((N_GDN, hv, D_MODEL), hv ** -0.5),
        'w_diff_qkv': nrm((N_DIFF, D_MODEL, 3 * D_MODEL), D_MODEL ** -0.5),
        'diff_lam': nrm((N_DIFF, 4, DIFF_DH), 0.1),
        'diff_subln': 1.0 + nrm((N_DIFF, 2 * DIFF_DH), 0.02),
        'w_diff_out': nrm((N_DIFF, D_MODEL, D_MODEL), D_MODEL ** -0.5),
        's5_a_re': -0.5 + nrm((N_S5, 2, S5_GROUPS, S5_STATE), 0.01),
        's5_a_im': s5_n + nrm((N_S5, 2, S5_GROUPS, S5_STATE), 0.01),
        's5_log_dt': unif((N_S5, 2, S5_GROUPS), math.log(1e-3), math.log(1e-1)),
        's5_b_re': nrm((N_S5, 2, S5_GROUPS, S5_STATE, S5_GROUP_CH), (2 * S5_GROUP_CH) ** -0.5),
        's5_b_im': nrm((N_S5, 2, S5_GROUPS, S5_STATE, S5_GROUP_CH), (2 * S5_GROUP_CH) ** -0.5),
        's5_c_re': nrm((N_S5, 2, S5_GROUPS, S5_GROUP_CH, S5_STATE), S5_STATE ** -0.5),
        's5_c_im': nrm((N_S5, 2, S5_GROUPS, S5_GROUP_CH, S5_STATE), S5_STATE ** -0.5),
        's5_d': nrm((N_S5, D_MODEL), 1.0),
        'w_s5_glu': nrm((N_S5, D_MODEL, 2 * D_MODEL), D_MODEL ** -0.5),
    }


def reference(x_prompt, x_sample, state_l0_gdn, cache_l1_k, cache_l1_v, state_l2_s5_re, state_l2_s5_im,
              state_l3_gdn, c, c_ctx, w_mod, b_mod, norm_gain, w_ffn_up, ffn_conv, w_ffn_down,
              w_gdn_qkv, gdn_conv, w_gdn_gate, w_gdn_alpha, w_gdn_beta, gdn_a_log, gdn_dt_bias, gdn_norm,
              w_gdn_out, w_diff_qkv, diff_lam, diff_subln, w_diff_out, s5_a_re, s5_a_im, s5_log_dt,
              s5_b_re, s5_b_im, s5_c_re, s5_c_im, s5_d, w_s5_glu):
    p = dict(w_mod=w_mod, b_mod=b_mod, norm_gain=norm_gain, w_ffn_up=w_ffn_up, ffn_conv=ffn_conv,
             w_ffn_down=w_ffn_down, w_gdn_qkv=w_gdn_qkv, gdn_conv=gdn_conv, w_gdn_gate=w_gdn_gate,
             w_gdn_alpha=w_gdn_alpha, w_gdn_beta=w_gdn_beta, gdn_a_log=gdn_a_log, gdn_dt_bias=gdn_dt_bias,
             gdn_norm=gdn_norm, w_gdn_out=w_gdn_out, w_diff_qkv=w_diff_qkv, diff_lam=diff_lam,
             diff_subln=diff_subln, w_diff_out=w_diff_out, s5_a_re=s5_a_re, s5_a_im=s5_a_im,
             s5_log_dt=s5_log_dt, s5_b_re=s5_b_re, s5_b_im=s5_b_im, s5_c_re=s5_c_re, s5_c_im=s5_c_im,
             s5_d=s5_d, w_s5_glu=w_s5_glu)
    y_prompt = x_prompt
    ctx_states = []
    for l in range(DEPTH):
        y_prompt, st = trunk_layer(p, l, y_prompt, c_ctx[None, :], None)
        ctx_states.append(st)
    caches = [(state_l0_gdn,), (cache_l1_k, cache_l1_v), (state_l2_s5_re, state_l2_s5_im), (state_l3_gdn,)]
    y_sample = x_sample
    for l in range(DEPTH):
        y_sample, _ = trunk_layer(p, l, y_sample, c, caches[l])
    (st0,), (k1, v1), (s2_re, s2_im), (st3,) = ctx_states
    return (y_prompt, y_sample, st0, k1, v1, s2_re, s2_im, st3)
```

```python
import math
import numpy as np
import concourse.bass as bass
import concourse.mybir as mybir
from concourse.bass_utils import run_bass_kernel_spmd

F32 = mybir.dt.float32
BF16 = mybir.dt.bfloat16
U8 = mybir.dt.uint8
AF = mybir.ActivationFunctionType
ALU = mybir.AluOpType
AX = mybir.AxisListType

NCORES = 8
D = 1024
NPROMPT = 1024
NSAMP = 4096
TOK = NPROMPT + NSAMP
SEQS = [(0, 256), (256, 256), (512, 256), (768, 256), (1024, 4096)]
FFN = 2816
EPS = 1e-6
DEPTH = 4

WEIGHT_NAMES = ['w_mod', 'b_mod', 'norm_gain', 'w_ffn_up', 'ffn_conv', 'w_ffn_down', 'w_gdn_qkv', 'gdn_conv',
                'w_gdn_gate', 'w_gdn_alpha', 'w_gdn_beta', 'gdn_a_log', 'gdn_dt_bias', 'gdn_norm', 'w_gdn_out',
                'w_diff_qkv', 'diff_lam', 'diff_subln', 'w_diff_out', 's5_a_re', 's5_a_im', 's5_log_dt',
                's5_b_re', 's5_b_im', 's5_c_re', 's5_c_im', 's5_d', 'w_s5_glu']


def seq_of(t):
    for s0, ln in SEQS:
        if s0 <= t < s0 + ln:
            return s0, s0 + ln
    raise ValueError(t)


class Buf:
    __slots__ = ('name', 'w', 'r', 'sem', 'cnt', 'group', 'psum')

    def __init__(self, name):
        self.name = name
        self.w = None
        self.r = {}
        self.sem = None
        self.cnt = 0
        self.group = None
        self.psum = False


class Prog:
    ENG = ['pe', 'act', 'dve', 'pool', 'sp']

    def __init__(self, nc):
        self.nc = nc
        self.ops = {e: [] for e in self.ENG}
        self.semh = {}
        self.cur = {}
        self.seen = {e: {} for e in self.ENG}
        for e in ['pe', 'act', 'dve', 'pool']:
            self.semh[e] = nc.alloc_semaphore('s_' + e)
            self.cur[e] = 0
        self.ndsem = 0
        self.free_dsems = []

    def _collect(self, eng, reads, writes, extra=()):
        waits = {}

        def need(tok):
            if tok is None:
                return
            k, v = tok
            if k not in ('pe', 'act', 'dve', 'pool'):
                v = max(v, self.cur[k])
            if k == 'pe' and eng == 'pe':
                return
            if waits.get(k, 0) < v:
                waits[k] = v
        for b in reads:
            need(b.w)
            if b.psum:
                for k, v in b.r.items():
                    if k != eng:
                        need((k, v))
        for b in writes:
            need(b.w)
            for k, v in b.r.items():
                need((k, v))
        for t in extra:
            need(t)
        wl = []
        seen = self.seen[eng]
        for k, v in waits.items():
            if seen.get(k, 0) < v:
                seen[k] = v
                wl.append((k, v))
        return wl

    def _mark(self, tok, reads, writes):
        k, v = tok
        for b in reads:
            if b.r.get(k, 0) < v:
                b.r[k] = v
        for b in writes:
            b.w = tok
            b.r = {}

    def op(self, eng, fn, reads=(), writes=()):
        wl = self._collect(eng, reads, writes)
        self.cur[eng] += 1
        tok = (eng, self.cur[eng])
        self.ops[eng].append((wl, fn, eng))
        self._mark(tok, reads, writes)
        return tok

    def dsem(self, b):
        if b.sem is None:
            if self.free_dsems:
                b.sem = self.free_dsems.pop()
            else:
                key = 'd%d' % self.ndsem
                self.ndsem += 1
                self.semh[key] = self.nc.alloc_semaphore(key)
                self.cur[key] = 0
                b.sem = key
        return b.sem

    def release(self, bufs):
        for b in bufs:
            if b.sem is not None:
                self.free_dsems.append(b.sem)
                b.sem = None

    def dma(self, q, out, in_, reads, writes, sb, group=None, **kw):
        key = self.dsem(sb)
        extra = []
        if self.cur[key] > 0 and (group is None or sb.group != group):
            extra.append((key, self.cur[key]))
        sb.group = group
        wl = self._collect(q, reads, writes, extra)
        self.cur[key] += 16
        tok = (key, self.cur[key])
        h = self.semh[key]

        def fn(e, out=out, in_=in_, h=h, kw=kw):
            e.dma_start(out=out, in_=in_, **kw).then_inc(h, 16)
        self.ops[q].append((wl, fn, None))
        self._mark(tok, reads, writes)
        return tok

    def barrier(self):
        for e in self.ENG:
            wl = []
            seen = self.seen[e]
            for k, v in self.cur.items():
                if k == e:
                    continue
                if v > 0 and seen.get(k, 0) < v:
                    seen[k] = v
                    wl.append((k, v))
            if wl:
                self.ops[e].append((wl, None, None))

    def replay(self):
        nc = self.nc
        engs = {'pe': 'tensor', 'act': 'scalar', 'dve': 'vector', 'pool': 'gpsimd', 'sp': 'sync'}
        with nc.Block() as block:
            for en, attr in engs.items():
                def body(e, en=en):
                    for wl, fn, inc in self.ops[en]:
                        if fn is None or inc is None or not wl or not ATTACH_WAIT:
                            for k, v in wl:
                                e.wait_ge(self.semh[k], v)
                            if fn is None:
                                continue
                            ins = fn(e)
                        else:
                            for k, v in wl[:-1]:
                                e.wait_ge(self.semh[k], v)
                            ins = fn(e)
                            k, v = wl[-1]
                            ins._wait_ge(self.semh[k], v)
                        if inc is not None:
                            ins.then_inc(self.semh[inc], 1)
                getattr(block, attr)(body)


class Arena:
    def __init__(self, nc):
        rem = nc.sbuf_bytes_remaining
        self.size = (rem // 1024 - 1) * 1024
        self.t = nc.alloc_sbuf_tensor('arena', [128, self.size], U8)
        self.off = 0
        self.stack = []

    def push(self):
        self.stack.append(self.off)

    def pop(self):
        self.off = self.stack.pop()

    def alloc(self, shape, dtype, name=None):
        esz = 4 if dtype == F32 else 2
        n = 1
        for s in shape:
            n *= s
        nb = (n * esz + 63) // 64 * 64
        assert self.off + nb <= self.size, ('SBUF arena overflow', name, self.off, nb, self.size)
        v = self.t[:, self.off:self.off + nb]
        self.off += nb
        v = v[:, 0:n * esz].bitcast(dtype)
        if len(shape) == 2:
            v = v.rearrange('p (a b) -> p a b', b=shape[1])
        elif len(shape) == 3:
            v = v.rearrange('p (a b c) -> p a b c', b=shape[1], c=shape[2])
        return v, Buf(name or 'buf')


class _PsumView:
    def __init__(self, ts, off=0):
        self.ts = ts
        self.off = off

    def __getitem__(self, key):
        prow, pcol = key
        a, b = pcol.start + self.off, pcol.stop + self.off
        i = a // 1024
        assert (b - 1) // 1024 == i, (a, b)
        return self.ts[i][prow, a - i * 1024:b - i * 1024]


class K:
    def __init__(self, parts=None, dbg=False):
        self.parts = parts
        self.nc = nc = bass.Bass("TRN2", target_bir_lowering=False)
        self.P = Prog(nc)
        self.A = Arena(nc)
        self.psum = _PsumView([nc.alloc_psum_tensor('psum%d' % i, [128, 1024], F32) for i in range(4)])
        self.pbuf = [Buf('ps%d' % i) for i in range(8)]
        for b in self.pbuf:
            b.psum = True
        self.din = {}
        self.dout = {}
        self.dbufs = {}

    def inp(self, name, shape):
        ap = self.nc.dram_tensor(name, list(shape), F32, kind="ExternalInput").ap()
        self.din[name] = ap
        return ap

    def outp(self, name, shape):
        ap = self.nc.dram_tensor(name, list(shape), F32, kind="ExternalOutput").ap()
        self.dout[name] = ap
        return ap

    def scratch(self, name, shape, dtype):
        return self.nc.dram_tensor(name, list(shape), dtype, kind="Internal").ap()

    def dbuf(self, key):
        if key not in self.dbufs:
            self.dbufs[key] = Buf(str(key))
        return self.dbufs[key]

    def bank(self, i, n=512, dtype=F32):
        v = self.psum[:, i * 512:i * 512 + (n if dtype == F32 else (n + 1) // 2)]
        if dtype != F32:
            v = v.bitcast(dtype)
        return v

    def mm(self, out, lhsT, rhs, start, stop, reads, writes, **kw):
        return self.P.op('pe', lambda e: e.matmul(out, lhsT, rhs, start=start, stop=stop, **kw), reads, writes)

    def tr(self, out, in_, ident, reads, writes):
        return self.P.op('pe', lambda e: e.transpose(out, in_, ident), reads, writes)

    def act(self, out, in_, func, reads, writes, bias=0.0, scale=1.0, accum_out=None, eng='act'):
        def fn(e):
            if accum_out is not None:
                return e.activation(out=out, in_=in_, func=func, bias=bias, scale=scale, accum_out=accum_out)
            return e.activation(out=out, in_=in_, func=func, bias=bias, scale=scale)
        return self.P.op('act', fn, reads, writes)

    def ts(self, eng, out, in0, s1, s2, op0, op1, reads, writes):
        def fn(e):
            if op1 is None:
                return e.tensor_scalar(out, in0, s1, None, op0)
            return e.tensor_scalar(out, in0, s1, s2, op0, op1)
        return self.P.op(eng, fn, reads, writes)

    def tt(self, eng, out, in0, in1, op, reads, writes):
        return self.P.op(eng, lambda e: e.tensor_tensor(out, in0, in1, op), reads, writes)

    def stt(self, eng, out, in0, scalar, in1, op0, op1, reads, writes):
        return self.P.op(eng, lambda e: e.scalar_tensor_tensor(out, in0, scalar, in1, op0, op1), reads, writes)

    def cp(self, eng, out, in_, reads, writes):
        if eng == 'act':
            return self.P.op('act', lambda e: e.activation(out=out, in_=in_, func=AF.Copy), reads, writes)
        return self.P.op(eng, lambda e: e.tensor_copy(out, in_), reads, writes)

    def rstd(self, s, b_s):
        n = s.shape[-1]
        self.ts('dve', s, s, EPS, None, ALU.add, None, [b_s], [b_s])
        self.tt('pool', s, s, self.mhalf[:, 0:n], ALU.pow, [b_s, self.b_epsc], [b_s])

    def memset(self, eng, out, val, writes):
        return self.P.op(eng, lambda e: e.memset(out, val), (), writes)

    def build(self):
        nc, P, A = self.nc, self.P, self.A
        i = self.inp
        self.xp = i('xp', [NPROMPT, D])
        self.xs = i('xs', [NSAMP, D])
        self.cond = i('cond', [2, D])
        self.st0_in = i('st0_in', [2, 8, 128, 128])
        self.ck_in = i('ck_in', [256, D])
        self.cv_in = i('cv_in', [256, D])
        self.s5re_in = i('s5re_in', [2, 64, 64])
        self.s5im_in = i('s5im_in', [2, 64, 64])
        self.st3_in = i('st3_in', [2, 8, 128, 128])
        self.c_ident = i('c_ident', [128, 128])
        self.c_masks = i('c_masks', [8, 128, 128])
        self.c_cos = i('c_cos', [128, NSAMP])
        self.c_lvl = i('c_lvl', [12, 128, 128])
        self.c_s5kv = i('c_s5kv', [2, 128, 201])
        self.c_s5m = i('c_s5m', [2, 128, 128])
        self.c_sin = i('c_sin', [128, NSAMP])
        self.w = {}
        shp = dict(w_mod=[4, D, 6 * D], b_mod=[4, 6 * D], norm_gain=[4, 4, D], w_ffn_up=[4, D, 2 * FFN],
                   ffn_conv=[4, 3, 2 * FFN], w_ffn_down=[4, FFN, D], w_gdn_qkv=[2, D, 3072], gdn_conv=[2, 3, 3072],
                   w_gdn_gate=[2, D, D], w_gdn_alpha=[2, D, 16], w_gdn_beta=[2, D, 16], gdn_a_log=[2, 2, 8],
                   gdn_dt_bias=[2, 2, 8], gdn_norm=[2, 128], w_gdn_out=[2, D, D], w_diff_qkv=[1, D, 3072],
                   diff_lam=[1, 4, 64], diff_subln=[1, 128], w_diff_out=[1, D, D], s5_a_re=[1, 2, 64, 64],
                   s5_a_im=[1, 2, 64, 64], s5_log_dt=[1, 2, 64], s5_b_re=[1, 2, 64, 64, 16], s5_b_im=[1, 2, 64, 64, 16],
                   s5_c_re=[1, 2, 64, 16, 64], s5_c_im=[1, 2, 64, 16, 64], s5_d=[1, D], w_s5_glu=[1, D, 2 * D])
        for n in WEIGHT_NAMES:
            self.w[n] = i(n, shp[n])
        o = self.outp
        self.yp = o('yp', [NPROMPT, D])
        self.ys = o('ys', [NSAMP, D])
        self.st0_out = o('st0_out', [4, 2, 8, 128, 128])
        self.k1_out = o('k1_out', [NPROMPT, D])
        self.v1_out = o('v1_out', [NPROMPT, D])
        self.s2re_out = o('s2re_out', [4, 2, 64, 64])
        self.s2im_out = o('s2im_out', [4, 2, 64, 64])
        self.st3_out = o('st3_out', [4, 2, 8, 128, 128])
        self.XA = self.scratch('XA', [TOK, D], F32)
        self.XB = self.scratch('XB', [TOK, D], F32)
        self.HTD = self.scratch('HTD', [8, 128, TOK], BF16)
        self.GROW = self.scratch('GROW', [DEPTH, 2, 2, D], F32)
        self.OTD = self.scratch('OTD', [8, 128, TOK], BF16)
        self.LAMD = self.scratch('LAMD', [1, 4], F32)
        self.HTD2 = self.scratch('HTD2', [8, 128, 8, 8 * 85], BF16)
        self.GTD = self.scratch('GTD', [8, 128, 8, 8 * 85], BF16)

        self.ident, self.b_ident = A.alloc([128], BF16, 'ident')
        self.modc, self.b_modc = A.alloc([DEPTH * 2 * 4 * 8], F32, 'modc')
        self.epsc, self.b_epsc = A.alloc([1], F32, 'epsc')
        self.mhalf, _ = A.alloc([512], F32, 'mhalf')
        self.emit_consts()
        self.prologue()
        parts = self.parts
        steps = [(l, k) for l in range(DEPTH) for k in ('mix', 'ffn') if parts is None or (l, k) in parts]
        cur = 'in'
        for n, (l, k) in enumerate(steps):
            last = n == len(steps) - 1
            if k == 'mix':
                dst = 'out' if last else 'XA'
                self.mixer_layer(l, cur, dst)
            else:
                dst = 'out' if last else 'XB'
                self.ffn_layer(l, cur, dst)
            cur = dst
        P.barrier()
        P.replay()
        return nc

    def xrows(self, which, t0, n):
        if which == 'in':
            ap = self.xp[t0:t0 + n, :] if t0 < NPROMPT else self.xs[t0 - NPROMPT:t0 - NPROMPT + n, :]
        elif which == 'out':
            ap = self.yp[t0:t0 + n, :] if t0 < NPROMPT else self.ys[t0 - NPROMPT:t0 - NPROMPT + n, :]
        elif which == 'XA':
            ap = self.XA[t0:t0 + n, :]
        else:
            ap = self.XB[t0:t0 + n, :]
        return ap, self.dbuf((which, t0 // 128))

    def emit_consts(self):
        A, P = self.A, self.P
        A.push()
        tmp, b_tmp = A.alloc([128], F32, 'ctmp')
        P.dma('sp', tmp, self.c_ident, [], [b_tmp], b_tmp)
        self.cp('dve', self.ident, tmp, [b_tmp], [self.b_ident])
        self.memset('dve', self.epsc, EPS, [self.b_epsc])
        self.memset('dve', self.mhalf, -0.5, [self.b_epsc])
        P.barrier()
        P.release([b_tmp])
        A.pop()

    def mc(self, l, grp, kind, fc=None):
        base = ((l * 2 + grp) * 4 + kind) * 8
        if fc is None:
            return self.modc[:, base:base + 8]
        return self.modc[:, base + fc:base + fc + 1]

    def prologue(self):
        A, P = self.A, self.P
        A.push()
        sct, b_sct = A.alloc([8, 2], BF16, 'scT')
        craw, b_craw = A.alloc([2, 8], F32, 'craw')
        bcol, b_bcol = A.alloc([DEPTH, 48], F32, 'bcol')
        gcol, b_gcol = A.alloc([DEPTH, 4, 8], F32, 'gcol')
        brow = [A.alloc([D], F32, 'brow%d' % k) for k in range(2)]
        grow = [A.alloc([D], F32, 'grow%d' % k) for k in range(2)]
        rowt = [A.alloc([D], F32, 'rowt%d' % k) for k in range(2)]
        wm = [A.alloc([8, 3 * D], BF16, 'wm%d' % k) for k in range(2)]
        nb = [b_craw, b_bcol, b_gcol] + [b for _, b in brow + grow + rowt + wm]
        kw = dict(allow_slow_non_contiguous=True)
        P.dma('sp', craw, self.cond.rearrange('g (k p) -> p g k', p=128), [], [b_craw], b_craw, **kw)
        P.dma('sp', bcol, self.w['b_mod'].rearrange('l (k p) -> p l k', p=128), [], [b_bcol], b_bcol, **kw)
        P.dma('sp', gcol, self.w['norm_gain'].rearrange('l s (k p) -> p l s k', p=128), [], [b_gcol], b_gcol, **kw)
        self.act(sct.rearrange('p k g -> p g k'), craw, AF.Silu, [b_craw], [b_sct])
        it = 0
        for l in range(DEPTH):
            for hf2 in range(2):
                k = it % 2
                it += 1
                wt, b_wt = wm[k]
                br, b_br = brow[k]
                gr, b_gr = grow[k]
                rt, b_rt = rowt[k]
                cb = hf2 * 3 * D
                for kc in range(8):
                    P.dma('pool', wt[:, kc, :], self.w['w_mod'][l, kc * 128:(kc + 1) * 128, cb:cb + 3 * D], [], [b_wt],
                          b_wt, group=('wm', l, hf2))
                P.dma('sp', br[0:2, :], self.w['b_mod'][l, cb + 2 * D:cb + 3 * D].partition_broadcast(2), [], [b_br], b_br)
                P.dma('sp', gr[0:2, :], self.w['norm_gain'][l, 2 * hf2 + 1, :].partition_broadcast(2), [], [b_gr], b_gr)
                pc = self.bank(k, 32).rearrange('p (a b c) -> p a b c', a=2, b=8)
                for a in range(2):
                    for fc in range(8):
                        c0 = a * D + fc * 128
                        for kc in range(8):
                            self.mm(pc[:, a, fc, :], wt[:, kc, c0:c0 + 128], sct[:, kc, :], kc == 0, kc == 7,
                                    [b_wt, b_sct], [self.pbuf[k]])
                pr = self.psum[0:2, (2 + 2 * k) * 512:(4 + 2 * k) * 512]
                for hf in range(2):
                    for kc in range(8):
                        self.mm(pr[:, hf * 512:(hf + 1) * 512], sct[:, kc, :],
                                wt[:, kc, 2 * D + hf * 512:2 * D + (hf + 1) * 512], kc == 0, kc == 7,
                                [b_wt, b_sct], [self.pbuf[2 + 2 * k + hf]])
                for grp in range(2):
                    dsh = self.mc(l, grp, 2 * hf2 + 1)
                    dsc = self.mc(l, grp, 2 * hf2)
                    self.tt('dve', dsh, pc[:, 0, :, grp], bcol[:, l, 24 * hf2:24 * hf2 + 8], ALU.add,
                            [self.pbuf[k], b_bcol], [self.b_modc])
                    self.stt('dve', dsc, pc[:, 1, :, grp], 1.0, bcol[:, l, 24 * hf2 + 8:24 * hf2 + 16], ALU.add, ALU.add,
                             [self.pbuf[k], b_bcol], [self.b_modc])
                    self.tt('dve', dsc, dsc, gcol[:, l, 2 * hf2, :], ALU.mult, [self.b_modc, b_gcol], [self.b_modc])
                self.tt('dve', rt[0:2, :], pr, br[0:2, :], ALU.add,
                        [self.pbuf[2 + 2 * k], self.pbuf[3 + 2 * k], b_br], [b_rt])
                self.tt('dve', rt[0:2, :], rt[0:2, :], gr[0:2, :], ALU.mult, [b_rt, b_gr], [b_rt])
                P.dma('sp', self.GROW[l, :, hf2, :], rt[0:2, :], [b_rt], [self.dbuf(('GROW', l))], b_rt)
        P.barrier()
        P.release(nb)
        A.pop()

    def norm_phase(self, l, site, src, dest, hT=None, b_hT=None):
        A, P = self.A, self.P
        kA = 0 if site == 0 else 2
        xt = [A.alloc([D], F32, 'nx%d' % k) for k in range(2)]
        xn = [A.alloc([D], BF16, 'nxn%d' % k) for k in range(2)]
        junk, b_junk = A.alloc([D], BF16, 'njunk')
        ss = [A.alloc([1], F32, 'nss%d' % k) for k in range(2)]
        ho = [A.alloc([8, 128], BF16, 'nho%d' % k) for k in range(2)]
        bufs = [b for _, b in xt + xn + ss + ho] + [b_junk]
        for ti in range(TOK // 128):
            t0 = ti * 128
            grp = 0 if t0 < NPROMPT else 1
            k = ti % 2
            x, b_x = xt[k]
            xb, b_xb = xn[k]
            s, b_s = ss[k]
            xap, b_src = self.xrows(src, t0, 128)
            P.dma('sp', x, xap, [b_src], [b_x], b_x)
            self.act(junk, x, AF.Square, [b_x], [b_junk, b_s], accum_out=s, scale=1.0 / 32.0)
            self.rstd(s, b_s)
            self.ts('dve', xb, x, s[:, 0:1], None, ALU.mult, None, [b_x, b_s], [b_xb])
            pb = 6 + k
            pt = self.bank(pb, 1024, BF16).rearrange('p (a b) -> p a b', a=8)
            for fc in range(8):
                self.tr(pt[:, fc, :], xb[:, fc * 128:(fc + 1) * 128], self.ident, [b_xb, self.b_ident], [self.pbuf[pb]])
            if dest == 'sbuf':
                for fc in range(8):
                    self.act(hT[:, fc, t0:t0 + 128], pt[:, fc, :], AF.Identity, [self.pbuf[pb], self.b_modc], [b_hT],
                             bias=self.mc(l, grp, kA + 1, fc), scale=self.mc(l, grp, kA, fc))
            else:
                h, b_h = ho[k]
                for fc in range(8):
                    self.act(h[:, fc, :], pt[:, fc, :], AF.Identity, [self.pbuf[pb], self.b_modc], [b_h],
                             bias=self.mc(l, grp, kA + 1, fc), scale=self.mc(l, grp, kA, fc))
                P.dma('sp', self.HTD[:, :, t0:t0 + 128].rearrange('k p t -> p k t'), h, [b_h],
                      [self.dbuf(('HTD', ti))], b_h)
        return bufs

    def ffn_layer(self, l, src, dst):
        A, P = self.A, self.P
        A.push()
        wup, b_wup = A.alloc([8, 2 * FFN], BF16, 'wup')
        wdn, b_wdn = A.alloc([22, D], BF16, 'wdn')
        cw, b_cw = A.alloc([3, 44], F32, 'ffn_cw')
        g2 = [A.alloc([D], F32, 'g2_%d' % g) for g in range(2)]
        allb = [b_wup, b_wdn, b_cw, g2[0][1], g2[1][1]]
        for kc in range(8):
            P.dma('pool', wup[:, kc, :], self.w['w_ffn_up'][l, kc * 128:(kc + 1) * 128, :], [], [b_wup], b_wup, group='wup')
        for jc in range(22):
            P.dma('pool', wdn[:, jc, :], self.w['w_ffn_down'][l, jc * 128:(jc + 1) * 128, :], [], [b_wdn], b_wdn, group='wdn')
        P.dma('sp', cw, self.w['ffn_conv'][l].rearrange('k (c p) -> p k c', p=128), [], [b_cw], b_cw,
              allow_slow_non_contiguous=True)
        for g in range(2):
            P.dma('sp', g2[g][0], self.GROW[l, g, 1, :].partition_broadcast(128), [self.dbuf(('GROW', l))], [g2[g][1]], g2[g][1])
        A.push()
        nb = self.norm_phase(l, 2, src, 'dram')
        P.barrier()
        P.release(nb)
        A.pop()
        W = 256
        hw = [A.alloc([8, W + 2], BF16, 'hw%d' % k) for k in range(2)]
        aT = [A.alloc([22, W], BF16, 'aT%d' % k) for k in range(2)]
        cg = [A.alloc([W], F32, 'cg%d' % k) for k in range(2)]
        cv = [A.alloc([W], F32, 'cv%d' % k) for k in range(2)]
        sg = [A.alloc([W], F32, 'sg%d' % k) for k in range(2)]
        xt = [A.alloc([D], F32, 'fx%d' % k) for k in range(2)]
        tm = [A.alloc([D], F32, 'ftm%d' % k) for k in range(2)]
        junk, b_junk = A.alloc([D], BF16, 'fjunk')
        ss = [A.alloc([1], F32, 'fss%d' % k) for k in range(2)]
        allb += [b for _, b in hw + aT + cg + cv + sg + xt + tm + ss] + [b_junk]
        cnt = 0
        sub_cnt = 0
        for wi in range(TOK // W):
            t0 = wi * W
            s0, s1 = seq_of(t0)
            lo, hi = max(t0 - 1, s0), min(t0 + W + 1, s1)
            ncol = hi - lo
            o = t0 - lo
            grp = 0 if t0 < NPROMPT else 1
            h, b_h = hw[wi % 2]
            a_, b_a = aT[wi % 2]
            P.dma('sp', h[:, :, 0:ncol], self.HTD[:, :, lo:hi].rearrange('k p t -> p k t'),
                  [self.dbuf(('HTD', ti)) for ti in range(lo // 128, (hi - 1) // 128 + 1)], [b_h], b_h)
            for jc in range(22):
                k = cnt % 2
                cnt += 1
                pg, pv = self.bank(2 * k, W + 2), self.bank(2 * k + 1, W + 2)
                bpg, bpv = self.pbuf[2 * k], self.pbuf[2 * k + 1]
                for (pp, bp, c0) in ((pg, bpg, jc * 128), (pv, bpv, FFN + jc * 128)):
                    for kc in range(8):
                        self.mm(pp[:, 0:ncol], wup[:, kc, c0:c0 + 128], h[:, kc, 0:ncol], kc == 0, kc == 7,
                                [b_wup, b_h], [bp])
                c_g, b_cg = cg[k]
                c_v, b_cv = cv[k]
                s_g, b_sg = sg[k]
                for (pp, bp, cc, b_cc, ch) in ((pg, bpg, c_g, b_cg, jc), (pv, bpv, c_v, b_cv, 22 + jc)):
                    self.act(cc, pp[:, o:o + W], AF.Copy, [bp, b_cw], [b_cc], scale=cw[:, 1, ch:ch + 1])
                    if t0 > s0:
                        self.stt('dve', cc, pp[:, o - 1:o - 1 + W], cw[:, 0, ch:ch + 1], cc, ALU.mult, ALU.add,
                                 [bp, b_cw, b_cc], [b_cc])
                    else:
                        self.stt('dve', cc[:, 1:W], pp[:, o:o + W - 1], cw[:, 0, ch:ch + 1], cc[:, 1:W], ALU.mult, ALU.add,
                                 [bp, b_cw, b_cc], [b_cc])
                    if t0 + W < s1:
                        self.stt('dve', cc, pp[:, o + 1:o + 1 + W], cw[:, 2, ch:ch + 1], cc, ALU.mult, ALU.add,
                                 [bp, b_cw, b_cc], [b_cc])
                    else:
                        self.stt('dve', cc[:, 0:W - 1], pp[:, o + 1:o + W], cw[:, 2, ch:ch + 1], cc[:, 0:W - 1], ALU.mult,
                                 ALU.add, [bp, b_cw, b_cc], [b_cc])
                self.act(s_g, c_g, AF.Silu, [b_cg], [b_sg])
                self.tt('pool', a_[:, jc, :], s_g, c_v, ALU.mult, [b_sg, b_cv], [b_a])
            for sub in range(W // 128):
                k = sub_cnt % 2
                sub_cnt += 1
                ts0 = t0 + sub * 128
                py = self.psum[:, (4 + 2 * k) * 512:(6 + 2 * k) * 512]
                bpy = [self.pbuf[4 + 2 * k], self.pbuf[5 + 2 * k]]
                for hf in range(2):
                    for jc in range(22):
                        self.mm(py[:, hf * 512:(hf + 1) * 512], a_[:, jc, sub * 128:(sub + 1) * 128],
                                wdn[:, jc, hf * 512:(hf + 1) * 512], jc == 0, jc == 21, [b_a, b_wdn], [bpy[hf]])
                x, b_x = xt[k]
                t_, b_t = tm[k]
                s_, b_s = ss[k]
                xap, b_src = self.xrows(src, ts0, 128)
                P.dma('sp', x, xap, [b_src], [b_x], b_x)
                self.act(junk, py, AF.Square, bpy, [b_junk, b_s], accum_out=s_, scale=1.0 / 32.0)
                self.rstd(s_, b_s)
                self.stt('dve', t_, py, s_[:, 0:1], g2[grp][0], ALU.mult, ALU.mult, bpy + [b_s, g2[grp][1]], [b_t])
                self.tt('pool', t_, t_, x, ALU.add, [b_t, b_x], [b_t])
                oap, b_dst = self.xrows(dst, ts0, 128)
                P.dma('sp', oap, t_, [b_t], [b_dst], b_t)
        P.barrier()
        P.release(allb)
        A.pop()

    def mixer_layer(self, l, src, dst):
        A, P = self.A, self.P
        kind = l % 3
        A.push()
        hT, b_hT = A.alloc([8, TOK], BF16, 'hT')
        A.push()
        nb = self.norm_phase(l, 0, src, 'sbuf', hT, b_hT)
        P.barrier()
        P.release(nb)
        A.pop()
        if kind == 0:
            self.gdn_core(l, hT, b_hT)
        elif kind == 1:
            self.attn_core(l, hT, b_hT)
        else:
            self.s5_convert(hT, b_hT)
        P.barrier()
        A.pop()
        if kind == 2:
            A.push()
            self.s5_core(l)
            P.barrier()
            A.pop()
        wname = {0: 'w_gdn_out', 1: 'w_diff_out', 2: 'w_s5_glu'}[kind]
        self.outproj_phase(l, self.w[wname][l // 3], kind == 2, src, dst)

    def outproj_phase(self, l, wap, glu, src, dst):
        A, P = self.A, self.P
        A.push()
        ncol = 2 * D if glu else D
        wo, b_wo = A.alloc([8, ncol], BF16, 'wo')
        for kc in range(8):
            P.dma('pool', wo[:, kc, :], wap[kc * 128:(kc + 1) * 128, :], [], [b_wo], b_wo, group='wo')
        g1 = [A.alloc([D], F32, 'g1_%d' % g) for g in range(2)]
        for g in range(2):
            P.dma('sp', g1[g][0], self.GROW[l, g, 0, :].partition_broadcast(128), [self.dbuf(('GROW', l))], [g1[g][1]], g1[g][1])
        ot = [A.alloc([8, 128], BF16, 'ot%d' % k) for k in range(2)]
        xt = [A.alloc([D], F32, 'ox%d' % k) for k in range(2)]
        tm = [A.alloc([D], F32, 'otm%d' % k) for k in range(2)]
        sgt = [A.alloc([D], F32, 'osg%d' % k) for k in range(2)]
        junk, b_junk = A.alloc([D], BF16, 'ojunk')
        ss = [A.alloc([1], F32, 'oss%d' % k) for k in range(2)]
        allb = [b_wo, g1[0][1], g1[1][1], b_junk] + [b for _, b in ot + xt + tm + sgt + ss]
        nbk = 4 if glu else 2
        for ti in range(TOK // 128):
            t0 = ti * 128
            grp = 0 if t0 < NPROMPT else 1
            k = ti % 2
            o_, b_o = ot[k]
            P.dma('sp', o_, self.OTD[:, :, t0:t0 + 128].rearrange('k p t -> p k t'), [self.dbuf(('OTD', ti))], [b_o], b_o)
            pb0 = k * 4
            py = _PsumView(self.psum.ts, pb0 * 512)
            bpy = [self.pbuf[pb0 + j] for j in range(nbk)]
            for hf in range(nbk):
                for kc in range(8):
                    self.mm(py[:, hf * 512:(hf + 1) * 512], o_[:, kc, :], wo[:, kc, hf * 512:(hf + 1) * 512], kc == 0, kc == 7,
                            [b_o, b_wo], [bpy[hf]])
            x, b_x = xt[k]
            t_, b_t = tm[k]
            s_, b_s = ss[k]
            xap, b_src = self.xrows(src, t0, 128)
            P.dma('sp', x, xap, [b_src], [b_x], b_x)
            if glu:
                sg_, b_sg = sgt[k]
                self.act(sg_, py[:, D:2 * D], AF.Sigmoid, bpy[2:4], [b_sg])
                self.tt('dve', sg_, sg_, py[:, 0:D], ALU.mult, [b_sg] + bpy[0:2], [b_sg])
                ysrc, ybufs = sg_, [b_sg]
            else:
                ysrc, ybufs = py[:, 0:D], bpy[0:2]
            self.act(junk, ysrc, AF.Square, ybufs, [b_junk, b_s], accum_out=s_, scale=1.0 / 32.0)
            self.rstd(s_, b_s)
            self.stt('dve', t_, ysrc, s_[:, 0:1], g1[grp][0], ALU.mult, ALU.mult, ybufs + [b_s, g1[grp][1]], [b_t])
            self.tt('pool', t_, t_, x, ALU.add, [b_t, b_x], [b_t])
            oap, b_dst = self.xrows(dst, t0, 128)
            P.dma('sp', oap, t_, [b_t], [b_dst], b_t)
        P.barrier()
        P.release(allb)
        A.pop()

    def attn_core(self, l, hT, b_hT):
        A, P = self.A, self.P
        lam_init = 0.8 - 0.6 * math.exp(-0.3 * l)
        wq = self.w['w_diff_qkv'][0]
        KT = 128
        cs, b_cs = A.alloc([2, NSAMP], F32, 'cossin')
        P.dma('sp', cs[:, 0, :], self.c_cos, [], [b_cs], b_cs, group='cs')
        P.dma('sp', cs[:, 1, :], self.c_sin, [], [b_cs], b_cs, group='cs')
        rotm, b_rotm = A.alloc([128], BF16, 'rotm')
        onesb, b_ones = A.alloc([128], BF16, 'onesb')
        rf, b_rf = A.alloc([128], F32, 'rotf')
        P.dma('sp', rf, self.c_masks[0], [], [b_rf], b_rf)
        self.cp('dve', rotm, rf, [b_rf], [b_rotm])
        self.memset('dve', onesb, 1.0, [b_ones])
        subl, b_subl = A.alloc([1], F32, 'subln')
        P.dma('sp', subl, self.w['diff_subln'][0].rearrange('(p o) -> p o', o=1), [], [b_subl], b_subl,
              allow_slow_non_contiguous=True)
        self.ts('dve', subl, subl, 1.0 - lam_init, None, ALU.mult, None, [b_subl], [b_subl])
        lamr, b_lamr = A.alloc([4, 64], F32, 'lamr')
        lams, b_lams = A.alloc([4], F32, 'lams')
        neglam, b_neglam = A.alloc([1], F32, 'neglam')
        onesf, b_onesf = A.alloc([128], F32, 'onesf')
        self.memset('dve', onesf, 1.0, [b_onesf])
        P.dma('sp', lamr[0:1], self.w['diff_lam'][0:1], [], [b_lamr], b_lamr)
        lr2 = lamr[0:1].rearrange('p (a b) f -> p a b f', b=2)
        self.tt('dve', lr2[:, :, 0, :], lr2[:, :, 0, :], lr2[:, :, 1, :], ALU.mult, [b_lamr], [b_lamr])
        self.P.op('dve', lambda e: e.reduce_sum(lams[0:1, 0:2], lr2[:, :, 0, :], AX.X), [b_lamr], [b_lams])
        self.act(lams[0:1, 0:2], lams[0:1, 0:2], AF.Exp, [b_lams], [b_lams])
        self.tt('dve', lams[0:1, 2:3], lams[0:1, 1:2], lams[0:1, 0:1], ALU.subtract, [b_lams], [b_lams])
        self.ts('dve', lams[0:1, 2:3], lams[0:1, 2:3], -lam_init, None, ALU.add, None, [b_lams], [b_lams])
        P.dma('sp', self.LAMD[0:1, :], lams[0:1, :], [b_lams], [self.dbuf('LAMD')], b_lams)
        P.dma('sp', neglam, self.LAMD[0, 2:3].partition_broadcast(128), [self.dbuf('LAMD')], [b_neglam], b_neglam)
        ckT, b_ckT = A.alloc([8, 256], BF16, 'ckT')
        cvb, b_cvb = A.alloc([2, D], BF16, 'cvb')
        A.push()
        ctmp = [A.alloc([D], F32, 'ctmp%d' % k) for k in range(2)]
        ctb, b_ctb = A.alloc([D], BF16, 'ctb')
        for kt in range(2):
            c_, b_c = ctmp[kt]
            P.dma('sp', c_, self.ck_in[kt * 128:(kt + 1) * 128, :], [], [b_c], b_c)
            self.cp('dve', ctb, c_, [b_c], [b_ctb])
            pt = self.bank(1, 1024, BF16).rearrange('p (a b) -> p a b', a=8)
            for h in range(8):
                self.tr(pt[:, h, :], ctb[:, h * 128:(h + 1) * 128], self.ident, [b_ctb, self.b_ident], [self.pbuf[1]])
            self.cp('dve', ckT[:, :, kt * 128:(kt + 1) * 128], pt, [self.pbuf[1]], [b_ckT])
        for kt in range(2):
            P.dma('pool', cvb[:, kt, :], self.cv_in[kt * 128:(kt + 1) * 128, :], [], [b_cvb], b_cvb, group='cv')
        P.barrier()
        P.release([ctmp[0][1], ctmp[1][1]])
        A.pop()
        wh, b_wh = A.alloc([8, 3, 128], BF16, 'wh')
        whf, b_whf = A.alloc([8, 3, 128], F32, 'whf')
        qT, b_qT = A.alloc([TOK], BF16, 'qT')
        kT, b_kT = A.alloc([TOK], BF16, 'kT')
        vt, b_vt = A.alloc([TOK // 128, 128], BF16, 'vt')
        t1, b_t1 = A.alloc([512], F32, 'at1')
        t2, b_t2 = A.alloc([512], F32, 'at2')
        pTs = [[A.alloc([512], BF16, 'pT%d%d' % (k, c)) for c in range(2)] for k in range(2)]
        of, b_of = A.alloc([512], F32, 'aof')
        o1, b_o1 = A.alloc([512], F32, 'ao1')
        osq, b_osq = A.alloc([512], BF16, 'aosq')
        ob, b_ob = A.alloc([512], BF16, 'aob')
        kvt = [A.alloc([D], F32, 'kvt%d' % k) for k in range(2)]
        rs, b_rs = A.alloc([512], F32, 'ars')
        it = 0
        wkv, b_wkv = A.alloc([8, 512], BF16, 'wkv')
        for part, oap in ((1, self.k1_out), (2, self.v1_out)):
            for hf in range(2):
                c0 = part * D + hf * 512
                for kc in range(8):
                    P.dma('pool', wkv[:, kc, :], wq[kc * 128:(kc + 1) * 128, c0:c0 + 512], [], [b_wkv], b_wkv, group=('wkv', part, hf))
                for ti in range(NPROMPT // 128):
                    k = it % 2
                    it += 1
                    for kc in range(8):
                        self.mm(self.bank(2 + k), hT[:, kc, ti * 128:(ti + 1) * 128], wkv[:, kc, :], kc == 0, kc == 7,
                                [b_hT, b_wkv], [self.pbuf[2 + k]])
                    kv_, b_kv = kvt[k]
                    self.cp('dve', kv_[:, 0:512], self.bank(2 + k), [self.pbuf[2 + k]], [b_kv])
                    P.dma('sp', oap[ti * 128:(ti + 1) * 128, hf * 512:(hf + 1) * 512], kv_[:, 0:512], [b_kv], [self.dbuf('kvout')], b_kv)
        for h in range(DBG.get('attn_heads', 8)):
            for part in range(3):
                P.dma('sp', whf[:, :, part, :], wq[:, part * D + h * 128:part * D + (h + 1) * 128].rearrange('(k p) c -> p k c', p=128),
                      [], [b_whf], b_whf, group=('wh', h))
            self.cp('dve', wh, whf, [b_whf], [b_wh])
            for tb in range(TOK // 512 if DBG.get('qk', 1) else 0):
                t0 = tb * 512
                for part, (dT, b_d) in ((0, (qT, b_qT)), (1, (kT, b_kT))):
                    pq = self.bank(0)
                    for kc in range(8):
                        self.mm(pq, wh[:, kc, part, :], hT[:, kc, t0:t0 + 512], kc == 0, kc == 7, [b_wh, b_hT], [self.pbuf[0]])
                    if t0 < NPROMPT or not DBG.get('rope', 1):
                        self.cp('act', dT[:, t0:t0 + 512], pq, [self.pbuf[0]], [b_d])
                    else:
                        ts_ = t0 - NPROMPT
                        self.cp('act', ob, pq, [self.pbuf[0]], [b_ob])
                        pr_ = self.bank(1)
                        self.mm(pr_, rotm, ob, True, True, [b_rotm, b_ob], [self.pbuf[1]])
                        self.tt('dve', t1, pq, cs[:, 0, ts_:ts_ + 512], ALU.mult, [self.pbuf[0], b_cs, b_ob], [b_t1])
                        self.tt('dve', t2, pr_, cs[:, 1, ts_:ts_ + 512], ALU.mult, [self.pbuf[1], b_cs], [b_t2])
                        self.tt('pool', dT[:, t0:t0 + 512], t1, t2, ALU.add, [b_t1, b_t2], [b_d])
            for ti in range(TOK // 128 if DBG.get('v', 1) else 0):
                k = ti % 2
                pv_ = self.bank(2 + k, 128)
                for kc in range(8):
                    self.mm(pv_, hT[:, kc, ti * 128:(ti + 1) * 128], wh[:, kc, 2, :], kc == 0, kc == 7, [b_hT, b_wh], [self.pbuf[2 + k]])
                self.cp('act', vt[:, ti, :], pv_, [self.pbuf[2 + k]], [b_vt])
            qtiles = [(s0, 256, [(kT, b_kT, s0 + j * 128, vt[:, (s0 // 128) + j, :], b_vt) for j in range(2)]) for s0 in (0, 256, 512, 768)]
            skeys = [(ckT, b_ckT, None, cvb[:, j, h * 128:(h + 1) * 128], b_cvb) for j in range(2)]
            skeys = [(ckT[:, h, j * 128:(j + 1) * 128], b_ckT, cvb[:, j, h * 128:(h + 1) * 128], b_cvb) for j in range(2)]
            skeys += [(kT[:, NPROMPT + j * 128:NPROMPT + (j + 1) * 128], b_kT, vt[:, NPROMPT // 128 + j, :], b_vt) for j in range(NSAMP // 128)]
            tiles = []
            for s0 in (0, 256, 512, 768):
                tiles.append((s0, 256, [(kT[:, s0 + j * 128:s0 + (j + 1) * 128], b_kT, vt[:, s0 // 128 + j, :], b_vt) for j in range(2)]))
            for qb in range(NSAMP // 512):
                tiles.append((NPROMPT + qb * 512, 512, skeys))
            for (q0, nq, keys) in tiles[:DBG.get('attn_tiles', 99)]:
                po = [self.bank(4, nq), self.bank(5, nq)]
                pz = [self.bank(6, nq), self.bank(7, nq)]
                nk = len(keys)
                for ki, (kap, b_k, vap, b_v) in enumerate(keys):
                    kk = ki % 2
                    for c in range(2):
                        psc = self.bank(2 * kk + c, nq)
                        self.mm(psc, kap[c * 64:(c + 1) * 64, :], qT[c * 64:(c + 1) * 64, q0:q0 + nq], True, True,
                                [b_k, b_qT], [self.pbuf[2 * kk + c]])
                        pT_, b_pT = pTs[kk][c]
                        self.act(pT_[:, 0:nq], psc, AF.Exp, [self.pbuf[2 * kk + c]], [b_pT], scale=0.125)
                    for c in range(2):
                        pT_, b_pT = pTs[kk][c]
                        self.mm(po[c], vap, pT_[:, 0:nq], ki == 0, ki == nk - 1, [b_v, b_pT], [self.pbuf[4 + c]])
                        self.mm(pz[c], onesb, pT_[:, 0:nq], ki == 0, ki == nk - 1, [b_ones, b_pT], [self.pbuf[6 + c]])
                self.P.op('dve', lambda e, a=rs[:, 0:nq], b=pz[0]: e.reciprocal(a, b), [self.pbuf[6]], [b_rs])
                self.tt('dve', of[:, 0:nq], po[0], rs[:, 0:nq], ALU.mult, [self.pbuf[4], b_rs], [b_of])
                self.P.op('dve', lambda e, a=rs[:, 0:nq], b=pz[1]: e.reciprocal(a, b), [self.pbuf[7]], [b_rs])
                self.tt('dve', o1[:, 0:nq], po[1], rs[:, 0:nq], ALU.mult, [self.pbuf[5], b_rs], [b_o1])
                self.stt('dve', of[:, 0:nq], o1[:, 0:nq], neglam[:, 0:1], of[:, 0:nq], ALU.mult, ALU.add, [b_o1, b_neglam, b_of], [b_of])
                self.act(osq[:, 0:nq], of[:, 0:nq], AF.Square, [b_of], [b_osq])
                self.mm(self.bank(6, nq), onesb, osq[:, 0:nq], True, True, [b_ones, b_osq], [self.pbuf[6]])
                self.ts('dve', rs[:, 0:nq], self.bank(6, nq), 1.0 / 128.0, EPS, ALU.mult, ALU.add, [self.pbuf[6]], [b_rs])
                self.tt('pool', rs[:, 0:nq], rs[:, 0:nq], self.mhalf[:, 0:nq], ALU.pow, [b_rs, self.b_epsc], [b_rs])
                self.stt('dve', ob[:, 0:nq], of[:, 0:nq], subl[:, 0:1], rs[:, 0:nq], ALU.mult, ALU.mult, [b_of, b_subl, b_rs], [b_ob])
                P.dma('sp', self.OTD[h, :, q0:q0 + nq], ob[:, 0:nq], [b_ob], [self.dbuf(('OTD', ti)) for ti in range(q0 // 128, (q0 + nq) // 128)], b_ob)

    def _zero_mixer(self):
        A, P = self.A, self.P
        z, b_z = A.alloc([TOK], BF16, 'zero_ot')
        self.memset('dve', z, 0.0, [b_z])
        for h in range(8):
            P.dma('sp', self.OTD[h], z, [b_z], [self.dbuf(('OTD', ti)) for ti in range(TOK // 128)], b_z, group='zot')

    def gdn_core(self, l, hT, b_hT):
        A, P = self.A, self.P
        j = l // 3
        st_in = self.st0_in if l == 0 else self.st3_in
        st_out = self.st0_out if l == 0 else self.st3_out
        NB = TOK // 128
        RS = 128.0 ** -0.5
        kw = dict(allow_slow_non_contiguous=True)
        mk32, b_mk32 = A.alloc([7, 128], F32, 'gmk32')
        P.dma('sp', mk32[:, 1:7, :], self.c_masks[1:7].rearrange('m p c -> p m c'), [], [b_mk32], b_mk32, group='mk')
        P.dma('sp', mk32[:, 0, :], self.c_ident, [], [b_mk32], b_mk32, group='mk')
        mkb, b_mkb = A.alloc([5, 128], BF16, 'gmkb')
        self.cp('dve', mkb[:, 0:3, :], mk32[:, 1:4, :], [b_mk32], [b_mkb])
        self.memset('dve', mkb[:, 3, :], 1.0, [b_mkb])
        self.cp('dve', mkb[:, 4, :], mk32[:, 6, :], [b_mk32], [b_mkb])
        ident32, NL, NU, csel = mk32[:, 0, :], mk32[:, 4, :], mk32[:, 5, :], mk32[:, 6, 0:2]
        lvl, b_lvl = A.alloc([12, 128], BF16, 'glvl')
        A.push()
        lv32, b_lv32 = A.alloc([12, 128], F32, 'glv32')
        P.dma('sp', lv32, self.c_lvl.rearrange('m p c -> p m c'), [], [b_lv32], b_lv32)
        self.cp('dve', lvl, lv32, [b_lv32], [b_lvl])
        P.barrier()
        P.release([b_lv32])
        A.pop()
        onesb, cselb = mkb[:, 3, :], mkb[:, 4, 0:2]
        dtb, b_dtb = A.alloc([16], F32, 'dtb')
        nA, b_nA = A.alloc([16], F32, 'nA')
        P.dma('sp', dtb, self.w['gdn_dt_bias'][j].rearrange('d h -> (d h)').partition_broadcast(128), [], [b_dtb], b_dtb)
        P.dma('sp', nA, self.w['gdn_a_log'][j].rearrange('d h -> (d h)').partition_broadcast(128), [], [b_nA], b_nA)
        self.act(nA, nA, AF.Exp, [b_nA], [b_nA])
        self.ts('dve', nA, nA, -1.0, None, ALU.mult, None, [b_nA], [b_nA])
        ngm, b_ngm = A.alloc([1], F32, 'gnormg')
        P.dma('sp', ngm, self.w['gdn_norm'][j].rearrange('(p o) -> p o', o=1), [], [b_ngm], b_ngm, **kw)
        cwc, b_cwc = A.alloc([3, 24], F32, 'gcw')
        P.dma('sp', cwc, self.w['gdn_conv'][j].rearrange('k (c p) -> p k c', p=128), [], [b_cwc], b_cwc, **kw)
        wab32, b_wab32 = A.alloc([8, 32], F32, 'wab32')
        P.dma('sp', wab32[:, :, 0:16], self.w['w_gdn_alpha'][j].rearrange('(k p) c -> p k c', p=128), [], [b_wab32], b_wab32, group='wab', **kw)
        P.dma('sp', wab32[:, :, 16:32], self.w['w_gdn_beta'][j].rearrange('(k p) c -> p k c', p=128), [], [b_wab32], b_wab32, group='wab', **kw)
        wab, b_wab = A.alloc([8, 32], BF16, 'wab')
        self.cp('dve', wab, wab32, [b_wab32], [b_wab])
        names = ['gc', 'be', 'kdA', 'kdB', 'nbeta']
        SC = {n: A.alloc([NB, 16], F32, 'sc_' + n) for n in names}
        SCb = {n: A.alloc([NB, 16], BF16, 'sc_' + n) for n in ('ghi', 'glo', 'bbf')}
        x16, b_x16 = A.alloc([16], F32, 'x16')
        y16, b_y16 = A.alloc([16], F32, 'y16')
        z16, b_z16 = A.alloc([16], F32, 'z16')
        g32, b_g32 = A.alloc([16], F32, 'g32')
        bt16, b_bt16 = A.alloc([1, 16], F32, 'bt16')
        for b in range(NB):
            t0 = b * 128
            pab = self.bank(0, 32)
            for kc in range(8):
                self.mm(pab, hT[:, kc, t0:t0 + 128], wab[:, kc, :], kc == 0, kc == 7, [b_hT, b_wab], [self.pbuf[0]])
            self.tt('dve', x16, pab[:, 0:16], dtb, ALU.add, [self.pbuf[0], b_dtb], [b_x16])
            self.act(y16, x16, AF.Abs, [b_x16], [b_y16])
            self.act(y16, y16, AF.Exp, [b_y16], [b_y16], scale=-1.0)
            self.act(y16, y16, AF.Ln, [b_y16], [b_y16], bias=1.0)
            self.ts('dve', z16, x16, 0.0, None, ALU.max, None, [b_x16], [b_z16])
            self.tt('dve', z16, z16, y16, ALU.add, [b_z16, b_y16], [b_z16])
            self.tt('dve', g32, z16, nA, ALU.mult, [b_z16, b_nA], [b_g32])
            bt, b_bt = bt16, b_bt16
            self.act(bt[:, 0, :], pab[:, 16:32], AF.Exp, [self.pbuf[0]], [b_bt], scale=-1.0)
            self.ts('dve', bt[:, 0, :], bt[:, 0, :], 1.0, None, ALU.add, None, [b_bt], [b_bt])
            self.P.op('dve', lambda e, o=bt[:, 0, :]: e.reciprocal(o, o), [b_bt], [b_bt])
            ghi, b_ghi = SCb['ghi']
            glo, b_glo = SCb['glo']
            bbf, b_bbf = SCb['bbf']
            self.cp('dve', ghi[:, b, :], g32, [b_g32], [b_ghi])
            self.tt('dve', glo[:, b, :], g32, ghi[:, b, :], ALU.subtract, [b_g32, b_ghi], [b_glo])
            self.cp('dve', bbf[:, b, :], bt[:, 0, :], [b_bt], [b_bbf])
            self.ts('dve', SC['nbeta'][0][:, b, :], bt[:, 0, :], -1.0, None, ALU.mult, None, [b_bt], [SC['nbeta'][1]])
            pg = self.bank(1, 48)
            for (c0, c1, tri) in ((0, 8, mkb[:, 0, :]), (8, 16, mkb[:, 1, :])):
                self.mm(pg[:, c0:c1], tri, ghi[:, b, c0:c1], True, False, [b_mkb, b_ghi], [self.pbuf[1]])
                self.mm(pg[:, c0:c1], tri, glo[:, b, c0:c1], False, True, [b_mkb, b_glo], [self.pbuf[1]])
            self.mm(pg[:, 16:32], mkb[:, 2, :], ghi[:, b, :], True, False, [b_mkb, b_ghi], [self.pbuf[1]])
            self.mm(pg[:, 16:32], mkb[:, 2, :], glo[:, b, :], False, True, [b_mkb, b_glo], [self.pbuf[1]])
            gc, b_gc = SC['gc']
            self.cp('dve', gc[:, b, :], pg[:, 0:16], [self.pbuf[1]], [b_gc])
            self.tt('dve', y16, pg[:, 16:32], gc[:, b, :], ALU.subtract, [self.pbuf[1], b_gc], [b_y16])
            self.act(y16, y16, AF.Exp, [b_y16], [b_y16])
            self.ts('dve', SC['kdA'][0][:, b, :], y16, csel[:, 0:1], None, ALU.mult, None, [b_y16, b_mk32], [SC['kdA'][1]])
            self.ts('dve', SC['kdB'][0][:, b, :], y16, csel[:, 1:2], None, ALU.mult, None, [b_y16, b_mk32], [SC['kdB'][1]])
            self.act(z16, gc[:, b, :], AF.Exp, [b_gc], [b_z16])
            self.tt('dve', SC['be'][0][:, b, :], z16, bt[:, 0, :], ALU.mult, [b_z16, b_bt], [SC['be'][1]])
        whf, b_whf = A.alloc([8, 128], F32, 'gwhf')
        wh, b_wh = A.alloc([8, 4, 128], BF16, 'gwh')
        qT, b_qT = A.alloc([TOK], BF16, 'gqT')
        kT, b_kT = A.alloc([TOK], BF16, 'gkT')
        vT, b_vT = A.alloc([TOK], BF16, 'gvT')
        gT, b_gT = A.alloc([TOK], BF16, 'ggT')
        oacc, b_oacc = A.alloc([TOK], F32, 'goacc')
        W = 256
        cc_ = [A.alloc([W], F32, 'gcv0%d' % k) for k in range(2)]
        s32_ = [A.alloc([W], F32, 'gcv1')] * 2
        sq_ = [A.alloc([W], BF16, 'gcv2')] * 2
        rr_ = [A.alloc([W], F32, 'gcv3')] * 2
        wqkv = self.w['w_gdn_qkv'][j]
        wg = self.w['w_gdn_gate'][j]

        def tile(shape, dt, name):
            return A.alloc(shape, dt, name)
        UB = []
        for d in range(2):
            u = {}
            for n, shp, dt in (('ta', [128], F32), ('dec', [128], F32), ('decT', [128], F32), ('tmp', [128], F32), ('ebc', [128], F32),
                               ('Nn', [128], BF16), ('Mm', [128], BF16), ('intraT', [128], BF16), ('qdT', [128], BF16),
                               ('gle', [2], F32), ('A0', [128], BF16), ('B0', [128], BF16), ('B1', [128], BF16),
                               ('R', [128], BF16), ('vb', [128], BF16), ('kbe', [128], BF16), ('kdA', [128], BF16), ('kdB', [128], BF16),
                               ('u32', [128], F32), ('wT', [128], BF16), ('vnew', [128], BF16), ('S32', [128], F32), ('Sbf', [128], BF16)):
                u[n] = A.alloc(shp, dt, 'u%d_%s' % (d, n))
            UB.append(u)
        for h in range(DBG.get('gdn_heads', 8)):
            for part in range(4):
                wsrc = wqkv[:, part * D + h * 128:part * D + (h + 1) * 128] if part < 3 else wg[:, h * 128:(h + 1) * 128]
                P.dma('sp', whf, wsrc.rearrange('(k p) c -> p k c', p=128), [], [b_whf], b_whf)
                self.cp('dve', wh[:, :, part, :], whf, [b_whf], [b_wh])
            self.memset('pool', oacc, 0.0, [b_oacc])
            it = 0
            for part, (dT, b_d) in enumerate(((qT, b_qT), (kT, b_kT), (vT, b_vT))):
                ch = part * 8 + h
                for wi in range(TOK // W):
                    t0 = wi * W
                    s0, s1 = seq_of(t0)
                    lo, hi = max(t0 - 1, s0), min(t0 + W + 1, s1)
                    ncol, o = hi - lo, t0 - max(t0 - 1, s0)
                    k = it % 2
                    it += 1
                    pp, bp = self.bank(2 + k, W + 2), self.pbuf[2 + k]
                    for kc in range(8):
                        self.mm(pp[:, 0:ncol], wh[:, kc, part, :], hT[:, kc, lo:hi], kc == 0, kc == 7, [b_wh, b_hT], [bp])
                    cc, b_cc = cc_[k]
                    self.act(cc, pp[:, o:o + W], AF.Copy, [bp, b_cwc], [b_cc], scale=cwc[:, 1, ch:ch + 1])
                    if t0 > s0:
                        self.stt('dve', cc, pp[:, o - 1:o - 1 + W], cwc[:, 0, ch:ch + 1], cc, ALU.mult, ALU.add, [bp, b_cwc, b_cc], [b_cc])
                    else:
                        self.stt('dve', cc[:, 1:W], pp[:, o:o + W - 1], cwc[:, 0, ch:ch + 1], cc[:, 1:W], ALU.mult, ALU.add, [bp, b_cwc, b_cc], [b_cc])
                    if t0 + W < s1:
                        self.stt('dve', cc, pp[:, o + 1:o + 1 + W], cwc[:, 2, ch:ch + 1], cc, ALU.mult, ALU.add, [bp, b_cwc, b_cc], [b_cc])
                    else:
                        self.stt('dve', cc[:, 0:W - 1], pp[:, o + 1:o + W], cwc[:, 2, ch:ch + 1], cc[:, 0:W - 1], ALU.mult, ALU.add, [bp, b_cwc, b_cc], [b_cc])
                    s32, b_s32 = s32_[k]
                    self.act(s32, cc, AF.Silu, [b_cc], [b_s32])
                    if part < 2:
                        sq, b_sq = sq_[k]
                        rr, b_rr = rr_[k]
                        self.act(sq, s32, AF.Square, [b_s32], [b_sq])
                        pn, bpn = self.bank(4 + k, W), self.pbuf[4 + k]
                        self.mm(pn, onesb, sq, True, True, [b_mkb, b_sq], [bpn])
                        self.ts('dve', rr, pn, EPS, None, ALU.add, None, [bpn], [b_rr])
                        self.tt('pool', rr, rr, self.mhalf[:, 0:W], ALU.pow, [b_rr, self.b_epsc], [b_rr])
                        self.tt('dve', dT[:, t0:t0 + W], s32, rr, ALU.mult, [b_s32, b_rr], [b_d])
                    else:
                        self.cp('pool', dT[:, t0:t0 + W], s32, [b_s32], [b_d])
            for tb in range(TOK // 512):
                t0 = tb * 512
                k = tb % 2
                pq, bq = self.bank(2 + k), self.pbuf[2 + k]
                for kc in range(8):
                    self.mm(pq, wh[:, kc, 3, :], hT[:, kc, t0:t0 + 512], kc == 0, kc == 7, [b_wh, b_hT], [bq])
                self.act(gT[:, t0:t0 + 512], pq, AF.Silu, [bq], [b_gT])
            chains = [self.gdn_chain(l, h, d, UB[d], SC, SCb, (qT, b_qT, kT, b_kT, vT, b_vT, oacc, b_oacc),
                                     (mk32, b_mk32, mkb, b_mkb, lvl, b_lvl), st_in, st_out) for d in range(2)]
            if DBG.get('gdn_dirs') is not None:
                chains = [chains[d] for d in DBG['gdn_dirs']]
            live = list(chains)
            if DBG.get('gdn_dirs') is None:
                pass
            while live:
                for g in list(live):
                    try:
                        next(g)
                    except StopIteration:
                        live.remove(g)
            self.gdn_out(h, oacc, b_oacc, gT, b_gT, ngm, b_ngm, onesb, b_mkb, sq_, rr_, cc_)

    def gdn_out(self, h, oacc, b_oacc, gT, b_gT, ngm, b_ngm, onesb, b_mkb, sqs, rrs, obs):
        P = self.P
        W = 256
        for tb in range(TOK // W):
            t0 = tb * W
            k = tb % 2
            sq, b_sq = sqs[k]
            rr, b_rr = rrs[k]
            ob, b_ob = obs[k]
            obb = ob.bitcast(BF16)[:, 0:W]
            self.act(sq, oacc[:, t0:t0 + W], AF.Square, [b_oacc], [b_sq])
            pn, bpn = self.bank(4 + k, W), self.pbuf[4 + k]
            self.mm(pn, onesb, sq, True, True, [b_mkb, b_sq], [bpn])
            self.ts('dve', rr, pn, 1.0 / 128.0, EPS, ALU.mult, ALU.add, [bpn], [b_rr])
            self.tt('pool', rr, rr, self.mhalf[:, 0:W], ALU.pow, [b_rr, self.b_epsc], [b_rr])
            self.stt('dve', rr, oacc[:, t0:t0 + W], ngm[:, 0:1], rr, ALU.mult, ALU.mult, [b_oacc, b_ngm, b_rr], [b_rr])
            self.tt('dve', obb, rr, gT[:, t0:t0 + W], ALU.mult, [b_rr, b_gT], [b_ob])
            P.dma('sp', self.OTD[h, :, t0:t0 + W], obb, [b_ob], [self.dbuf(('OTD', ti)) for ti in range(t0 // 128, t0 // 128 + 2)], b_ob)

    def gdn_chain(self, l, h, d, u, SC, SCb, bufs, consts, st_in, st_out):
        P = self.P
        RS = 128.0 ** -0.5
        qT, b_qT, kT, b_kT, vT, b_vT, oacc, b_oacc = bufs
        mk32, b_mk32, mkb, b_mkb, lvl, b_lvl = consts
        ident32, NL, NU = mk32[:, 0, :], mk32[:, 4, :], mk32[:, 5, :]
        cselb = mkb[:, 4, 0:2]
        NLd, NUd = (NL, NU) if d == 0 else (NU, NL)
        tri = mkb[:, d, :]
        dh = d * 8 + h
        pb = 4 * d
        B0, B1, B2, B3 = [self.bank(pb + i) for i in range(4)]
        P0, P1, P2, P3 = [self.pbuf[pb + i] for i in range(4)]
        T = lambda n: u[n][0]
        Bf = lambda n: u[n][1]
        ghi, b_ghi = SCb['ghi']
        glo, b_glo = SCb['glo']
        bbf, b_bbf = SCb['bbf']
        col = lambda n, b: SC[n][0][:, b, dh:dh + 1]
        self.memset('dve', T('vnew'), 0.0, [Bf('vnew')])
        for si, (s0, ln) in enumerate(SEQS):
            blocks = list(range(s0 // 128, (s0 + ln) // 128))
            if d == 1:
                blocks = blocks[::-1]
            if si < 4:
                self.memset('dve', T('S32'), 0.0, [Bf('S32')])
            else:
                P.dma('sp', T('S32'), st_in[d, h], [], [Bf('S32')], Bf('S32'))
            self.cp('act', T('Sbf'), T('S32'), [Bf('S32')], [Bf('Sbf')])
            for b in blocks:
                t0 = b * 128
                kb, qb, vb_ = kT[:, t0:t0 + 128], qT[:, t0:t0 + 128], vT[:, t0:t0 + 128]
                ghb = ghi[:, b, dh:dh + 1].to_broadcast([128, 128])
                glb = glo[:, b, dh:dh + 1].to_broadcast([128, 128])
                bbb = bbf[:, b, dh:dh + 1].to_broadcast([128, 128])
                self.mm(B0[:, 0:128], ghb, tri, True, False, [b_ghi, b_mkb], [P0])
                self.mm(B0[:, 0:128], glb, tri, False, True, [b_glo, b_mkb], [P0])
                self.mm(B0[:, 128:256], bbb, self.ident, True, True, [b_bbf, self.b_ident], [P0])
                self.mm(B0[:, 256:258], ghb, cselb, True, False, [b_ghi, b_mkb], [P0])
                self.mm(B0[:, 256:258], glb, cselb, False, True, [b_glo, b_mkb], [P0])
                self.mm(B1[:, 0:128], kb, kb, True, True, [b_kT], [P1])
                self.mm(B1[:, 128:256], kb, qb, True, True, [b_kT, b_qT], [P1])
                self.stt('dve', T('ta'), B0[:, 0:128], -1.0, NLd, ALU.mult, ALU.add, [P0, b_mk32], [Bf('ta')])
                self.act(T('dec'), T('ta'), AF.Exp, [Bf('ta'), SC['gc'][1]], [Bf('dec')], bias=col('gc', b))
                self.stt('dve', T('ta'), B0[:, 0:128], col('gc', b), NUd, ALU.subtract, ALU.add, [P0, b_mk32, SC['gc'][1]], [Bf('ta')])
                self.act(T('decT'), T('ta'), AF.Exp, [Bf('ta')], [Bf('decT')])
                self.act(T('ebc'), B0[:, 0:128], AF.Exp, [P0], [Bf('ebc')])
                self.act(T('gle'), B0[:, 256:258], AF.Exp, [P0], [Bf('gle')])
                self.stt('dve', T('Nn'), B1[:, 0:128], col('nbeta', b), T('dec'), ALU.mult, ALU.mult, [P1, SC['nbeta'][1], Bf('dec')], [Bf('Nn')])
                self.tt('dve', T('tmp'), T('decT'), B0[:, 128:256], ALU.mult, [Bf('decT'), P0], [Bf('tmp')])
                self.stt('dve', T('Mm'), B1[:, 0:128], -1.0, T('tmp'), ALU.mult, ALU.mult, [P1, Bf('tmp')], [Bf('Mm')])
                self.tt('pool', T('decT'), T('decT'), ident32, ALU.add, [Bf('decT'), b_mk32, Bf('tmp')], [Bf('decT')])
                self.stt('dve', T('intraT'), B1[:, 128:256], RS, T('decT'), ALU.mult, ALU.mult, [P1, Bf('decT')], [Bf('intraT')])
                self.stt('dve', T('qdT'), qb, RS, T('ebc'), ALU.mult, ALU.mult, [b_qT, Bf('ebc')], [Bf('qdT')])
                mX = lvl[:, 0:6, :] if d == 0 else lvl[:, 6:12, :]
                mW = lvl[:, 6:12, :] if d == 0 else lvl[:, 0:6, :]
                self.tt('pool', T('R'), T('Mm'), mX[:, 0, :], ALU.mult, [Bf('Mm'), b_lvl], [Bf('R')])
                self.tt('pool', T('R'), T('R'), ident32, ALU.add, [Bf('R'), b_mk32], [Bf('R')])
                self.tt('pool', T('A0'), T('Nn'), mW[:, 0, :], ALU.mult, [Bf('Nn'), b_lvl], [Bf('A0')])
                self.tt('pool', T('A0'), T('A0'), ident32, ALU.add, [Bf('A0'), b_mk32], [Bf('A0')])
                for k in range(1, 6):
                    last = k == 5
                    self.mm(B2[:, 0:128], T('Nn'), T('R'), True, True, [Bf('Nn'), Bf('R')], [P2])
                    if not last:
                        self.mm(B2[:, 128:256], T('Mm'), T('A0'), True, True, [Bf('Mm'), Bf('A0')], [P2])
                    self.cp('act', T('B0'), B2[:, 0:128], [P2], [Bf('B0')])
                    if not last:
                        self.cp('act', T('B1'), B2[:, 128:256], [P2], [Bf('B1')])
                    self.mm(B2[:, 256:384], T('A0'), T('B0'), True, True, [Bf('A0'), Bf('B0')], [P2])
                    if not last:
                        self.mm(B2[:, 384:512], T('R'), T('B1'), True, True, [Bf('R'), Bf('B1')], [P2])
                    self.tt('dve', T('tmp'), B2[:, 256:384], mX[:, k, :], ALU.mult, [P2, b_lvl], [Bf('tmp')])
                    if not last:
                        self.tt('dve', T('ta'), B2[:, 384:512], mW[:, k, :], ALU.mult, [P2, b_lvl], [Bf('ta')])
                    self.tt('pool', T('R'), T('R'), T('tmp'), ALU.add, [Bf('R'), Bf('tmp')], [Bf('R')])
                    if not last:
                        self.tt('pool', T('A0'), T('A0'), T('ta'), ALU.add, [Bf('A0'), Bf('ta')], [Bf('A0')])
                ptr = B3[:, 0:128].bitcast(BF16)
                self.tr(ptr[:, 0:128], vb_, self.ident, [b_vT, self.b_ident], [P3])
                self.tr(ptr[:, 128:256], kb, self.ident, [b_kT, self.b_ident], [P3])
                self.ts('dve', T('vb'), ptr[:, 0:128], col('nbeta', b), -1.0, ALU.mult, ALU.mult, [P3, SC['nbeta'][1]], [Bf('vb')])
                self.ts('dve', T('kbe'), ptr[:, 128:256], col('be', b), None, ALU.mult, None, [P3, SC['be'][1]], [Bf('kbe')])
                self.ts('dve', T('kdA'), ptr[:, 128:256], col('kdA', b), None, ALU.mult, None, [P3, SC['kdA'][1]], [Bf('kdA')])
                self.ts('dve', T('kdB'), ptr[:, 128:256], col('kdB', b), None, ALU.mult, None, [P3, SC['kdB'][1]], [Bf('kdB')])
                self.mm(B3[:, 128:256], T('R'), T('vb'), True, True, [Bf('R'), Bf('vb')], [P3])
                self.mm(B3[:, 256:384], T('kbe'), T('R'), True, True, [Bf('R'), Bf('kbe')], [P3])
                self.cp('act', T('u32'), B3[:, 128:256], [P3], [Bf('u32')])
                self.cp('act', T('wT'), B3[:, 256:384], [P3], [Bf('wT')])
                yield
                for c in ((0, 1) if d == 0 else (1, 0)):
                    rows = slice(c * 64, (c + 1) * 64)
                    cols = slice(c * 64, (c + 1) * 64)
                    self.mm(B3[:, 384:512], T('wT'), T('Sbf'), True, True, [Bf('wT'), Bf('Sbf')], [P3])
                    self.tt('dve', T('vnew')[rows, :], T('u32')[rows, :], B3[rows, 384:512], ALU.subtract, [Bf('u32'), P3], [Bf('vnew')])
                    self.mm(B2[:, 384:448], T('Sbf'), T('qdT')[:, cols], True, False, [Bf('Sbf'), Bf('qdT')], [P2])
                    self.mm(B2[:, 384:448], T('vnew'), T('intraT')[:, cols], False, True, [Bf('vnew'), Bf('intraT')], [P2])
                    oc = oacc[:, t0 + c * 64:t0 + (c + 1) * 64]
                    self.tt('dve', oc, oc, B2[:, 384:448], ALU.add, [P2, b_oacc], [b_oacc])
                    kd = 'kdA' if c == 0 else 'kdB'
                    self.mm(B1[:, 256:384], T(kd), T('vnew'), True, True, [Bf(kd), Bf('vnew')], [P1])
                    self.stt('dve', T('S32'), T('S32'), T('gle')[:, c:c + 1], B1[:, 256:384], ALU.mult, ALU.add,
                             [Bf('S32'), Bf('gle'), P1], [Bf('S32')])
                    self.cp('act', T('Sbf'), T('S32'), [Bf('S32')], [Bf('Sbf')])
                    yield
            if si < 4:
                P.dma('sp', st_out[si, d, h], T('S32'), [Bf('S32')], [self.dbuf(('st', l))], Bf('S32'))

    NSL = 85

    @staticmethod
    def s5_slot(n):
        return n + n // 4 + 1 if n < 16 else n + 5

    def s5_convert(self, hT, b_hT):
        A, P = self.A, self.P
        A.push()
        stg = [A.alloc([8, 8, self.NSL], BF16, 's5stg%d' % k) for k in range(2)]
        for s_, b_ in stg:
            self.memset('dve', s_, 0.0, [b_])
        for fc in range(8):
            s_, b_ = stg[fc % 2]
            for (s0, ln) in SEQS:
                n0, nch = s0 // 64, ln // 64
                c0 = self.s5_slot(n0)
                src_ = hT[:, fc, s0:s0 + ln].rearrange('p (n J j) -> p j J n', J=8, j=8)
                self.cp('dve' if fc % 2 == 0 else 'pool', s_[:, :, :, c0:c0 + nch], src_, [b_hT], [b_])
            P.dma('sp', self.HTD2[fc], s_.rearrange('p j J n -> p j (J n)'), [b_], [self.dbuf(('HTD2', fc))], b_)
        P.barrier()
        P.release([b for _, b in stg])
        A.pop()

    def s5_core(self, l):
        A, P = self.A, self.P
        NSL = self.NSL
        GD = 64
        kw = dict(allow_slow_non_contiguous=True)
        W = self.w
        TWO_PI = 2.0 * math.pi
        MAGIC = 12582912.0
        kv, b_kv = A.alloc([2, 201], F32, 's5kv')
        P.dma('sp', kv, self.c_s5kv.rearrange('d p k -> p d k'), [], [b_kv], b_kv)
        mfb32, b_mfb32 = A.alloc([2, 128], F32, 's5mfb32')
        P.dma('sp', mfb32, self.c_s5m.rearrange('d p k -> p d k'), [], [b_mfb32], b_mfb32)
        identb = self.ident
        dU, b_dU = A.alloc([64], F32, 's5dU')
        for jl in range(8):
            P.dma('sp', dU[jl * 16:(jl + 1) * 16, :], W['s5_d'][0].rearrange('(g c) -> c g', c=16), [], [b_dU], b_dU, group='dU', **kw)
        prm = {n: A.alloc([GD], F32, 's5p_' + n) for n in ('are', 'aim', 'thr', 'thi', 'a64r', 'a64i')}
        for d in range(2):
            for nm, key in (('are', 's5_a_re'), ('aim', 's5_a_im')):
                P.dma('sp', prm[nm][0][:, d * 32:(d + 1) * 32], W[key][0, d].rearrange('(gp gi) p -> (gi p) gp', gi=2), [], [prm[nm][1]],
                      prm[nm][1], group=nm, **kw)
            for gi in range(2):
                P.dma('sp', prm['thr'][0][gi * 64:(gi + 1) * 64, d * 32:(d + 1) * 32],
                      W['s5_log_dt'][0, d].rearrange('(gp gi) -> gi gp', gi=2)[gi].partition_broadcast(64), [], [prm['thr'][1]],
                      prm['thr'][1], group='ldt', **kw)
        pt = lambda n: prm[n][0]
        pb = lambda n: prm[n][1]
        self.act(pt('thr'), pt('thr'), AF.Exp, [pb('thr')], [pb('thr')])
        self.tt('dve', pt('thi'), pt('thr'), pt('aim'), ALU.mult, [pb('thr'), pb('aim')], [pb('thi')])
        self.tt('dve', pt('thr'), pt('thr'), pt('are'), ALU.mult, [pb('thr'), pb('are')], [pb('thr')])
        Bt = {x: A.alloc([GD, 16], F32, 's5B' + x) for x in ('r', 'i')}
        for d in range(2):
            for x, key in (('r', 's5_b_re'), ('i', 's5_b_im')):
                P.dma('sp', Bt[x][0][:, d * 32:(d + 1) * 32, :], W[key][0, d].rearrange('(gp gi) p c -> (gi p) gp c', gi=2), [], [Bt[x][1]],
                      Bt[x][1], group='B' + x, **kw)
        Ct = {x: A.alloc([GD, 16], F32, 's5C' + x) for x in ('r', 'i')}
        A.push()
        cn32, b_cn32 = A.alloc([8, 128], F32, 's5cn32')
        cnb, b_cnb = A.alloc([8, 128], BF16, 's5cnb')
        for x, key in (('r', 's5_c_re'), ('i', 's5_c_im')):
            for gl in range(8):
                for d in range(2):
                    for gh in range(4):
                        srcap = W[key][0, d].rearrange('(gh gl gi) c p -> gh gl c gi p', gl=8, gi=2)[gh, gl]
                        P.dma('sp', cn32[gl * 16:(gl + 1) * 16, d * 4 + gh, :].rearrange('c (gi p) -> c gi p', gi=2), srcap,
                              [], [b_cn32], b_cn32, group=('cn', x), **kw)
            self.cp('dve', cnb, cn32, [b_cn32], [b_cnb])
            ptc = self.bank(0, 1024, BF16).rearrange('p (a b) -> p a b', a=8)
            for q in range(8):
                self.tr(ptc[:, q, :], cnb[:, q, :], self.ident, [b_cnb, self.b_ident], [self.pbuf[0]])
            self.cp('dve', Ct[x][0].rearrange('p (q gl) c -> p q gl c', gl=8), ptc.rearrange('p q (gl c) -> p q gl c', c=16),
                    [self.pbuf[0]], [Ct[x][1]])
        P.barrier()
        P.release([b_cn32])
        A.pop()
        ST = {x: A.alloc([GD, NSL], F32, 's5ST' + x) for x in ('r', 'i')}
        for x in ('r', 'i'):
            self.memset('pool', ST[x][0], 0.0, [ST[x][1]])
        pw = {x: A.alloc([201], F32, 's5pw' + x) for x in ('r', 'i')}
        ang, b_ang = A.alloc([201], F32, 's5ang')
        an2, b_an2 = A.alloc([201], F32, 's5an2')
        tmpA = [A.alloc([64, 16], F32, 's5tA%d' % k) for k in range(2)]
        tmpB = [A.alloc([64, 16], F32, 's5tB%d' % k) for k in range(2)]
        fr, b_fr = A.alloc([4], F32, 's5f')
        bb = {x: A.alloc([16], F32, 's5bb' + x) for x in ('r', 'i')}
        ubig = [A.alloc([8, 8 * NSL], BF16, 's5U%d' % k) for k in range(2)]

        def load_u(fc, k):
            u_, b_u = ubig[k]
            for jl in range(8):
                P.dma('sp', u_[jl * 16:(jl + 1) * 16, :, :], self.HTD2[fc, :, jl, :].rearrange('(gl c) x -> c gl x', c=16),
                      [self.dbuf(('HTD2', fc))], [b_u], b_u, group=('u', fc))
            return u_, b_u

        def gen_pw(gd, d):
            kvd = kv[:, d, :]
            thr, thi = pt('thr')[:, gd:gd + 1], pt('thi')[:, gd:gd + 1]
            self.act(pw['r'][0], kvd, AF.Exp, [b_kv, pb('thr')], [pw['r'][1]], scale=thr)
            for (shift, key) in ((0.0, 'i'), (0.5 * math.pi, 'c')):
                self.ts('dve', ang, kvd, thi, shift, ALU.mult, ALU.add, [b_kv, pb('thi')], [b_ang])
                self.ts('dve', an2, ang, 1.0 / TWO_PI, MAGIC, ALU.mult, ALU.add, [b_ang], [b_an2])
                self.ts('dve', an2, an2, -MAGIC, None, ALU.add, None, [b_an2], [b_an2])
                self.stt('dve', ang, an2, -TWO_PI, ang, ALU.mult, ALU.add, [b_an2, b_ang], [b_ang])
                self.ts('dve', ang, ang, math.pi, -math.pi, ALU.min, ALU.max, [b_ang], [b_ang])
                if key == 'i':
                    self.act(pw['i'][0], ang, AF.Sin, [b_ang], [pw['i'][1]])
                else:
                    self.act(an2, ang, AF.Sin, [b_ang], [b_an2])
            self.tt('dve', pw['i'][0], pw['i'][0], pw['r'][0], ALU.mult, [pw['i'][1], pw['r'][1]], [pw['i'][1]])
            self.tt('dve', pw['r'][0], pw['r'][0], an2, ALU.mult, [pw['r'][1], b_an2], [pw['r'][1]])
            are, aim = pt('are')[:, gd:gd + 1], pt('aim')[:, gd:gd + 1]
            f = fr
            self.tt('dve', f[:, 0:1], are, are, ALU.mult, [pb('are')], [b_fr])
            self.stt('dve', f[:, 0:1], aim, aim, f[:, 0:1], ALU.mult, ALU.add, [pb('aim'), b_fr], [b_fr])
            self.P.op('dve', lambda e: e.reciprocal(f[:, 0:1], f[:, 0:1]), [b_fr], [b_fr])
            self.ts('dve', f[:, 1:2], pw['r'][0][:, 1:2], -1.0, None, ALU.add, None, [pw['r'][1]], [b_fr])
            self.tt('dve', f[:, 2:3], f[:, 1:2], are, ALU.mult, [b_fr, pb('are')], [b_fr])
            self.stt('dve', f[:, 2:3], pw['i'][0][:, 1:2], aim, f[:, 2:3], ALU.mult, ALU.add, [pw['i'][1], pb('aim'), b_fr], [b_fr])
            self.tt('dve', f[:, 3:4], f[:, 1:2], aim, ALU.mult, [b_fr, pb('aim')], [b_fr])
            self.stt('dve', f[:, 3:4], pw['i'][0][:, 1:2], are, f[:, 3:4], ALU.mult, ALU.subtract, [pw['i'][1], pb('are'), b_fr], [b_fr])
            self.ts('dve', f[:, 2:4], f[:, 2:4], f[:, 0:1], None, ALU.mult, None, [b_fr], [b_fr])
            br_, bi_ = Bt['r'][0][:, gd, :], Bt['i'][0][:, gd, :]
            rb = [Bt['r'][1], Bt['i'][1], b_fr]
            self.ts('dve', bb['r'][0], br_, f[:, 2:3], None, ALU.mult, None, rb, [bb['r'][1]])
            self.stt('dve', bb['r'][0], bi_, f[:, 3:4], bb['r'][0], ALU.mult, ALU.subtract, rb + [bb['r'][1]], [bb['r'][1]])
            self.ts('dve', bb['r'][0], bb['r'][0], -1.0, None, ALU.mult, None, [bb['r'][1]], [bb['r'][1]])
            self.ts('dve', bb['i'][0], bi_, f[:, 2:3], None, ALU.mult, None, rb, [bb['i'][1]])
            self.stt('dve', bb['i'][0], br_, f[:, 3:4], bb['i'][0], ALU.mult, ALU.add, rb + [bb['i'][1]], [bb['i'][1]])

        def ctab(out_r, b_or, out_i, b_oi, k0, K, Mr, Mi, mbufs, neg_im):
            pr = pw['r'][0][:, k0:k0 + K].unsqueeze(2).to_broadcast([128, K, 16])
            pi_ = pw['i'][0][:, k0:k0 + K].unsqueeze(2).to_broadcast([128, K, 16])
            mr = Mr.unsqueeze(1).to_broadcast([128, K, 16])
            mi = Mi.unsqueeze(1).to_broadcast([128, K, 16])
            rd = [pw['r'][1], pw['i'][1]] + mbufs
            (ta, b_ta), (tb, b_tb) = tmpA[0], tmpA[1]
            (tc_, b_tc), (td, b_td) = tmpB[0], tmpB[1]
            ta, tb, tc_, td = ta[:, 0:K, :], tb[:, 0:K, :], tc_[:, 0:K, :], td[:, 0:K, :]
            self.tt('dve', ta, pr, mr, ALU.mult, rd, [b_ta])
            self.tt('dve', tb, pi_, mi, ALU.mult, rd, [b_tb])
            self.tt('dve', out_r, ta, tb, ALU.subtract, [b_ta, b_tb], [b_or])
            self.tt('pool', tc_, pr, mi, ALU.mult, rd, [b_tc])
            self.tt('pool', td, pi_, mr, ALU.mult, rd, [b_td])
            if neg_im:
                self.stt('dve', out_i, tc_, -1.0, td, ALU.mult, ALU.subtract, [b_tc, b_td], [b_oi])
            else:
                self.tt('pool', out_i, tc_, td, ALU.add, [b_tc, b_td], [b_oi])

        KS = {x: A.alloc([64, 16], BF16, 's5KS' + x) for x in ('r', 'i')}
        KST = {x: A.alloc([8, 128], BF16, 's5KST' + x) for x in ('r', 'i')}
        for fc in range(DBG.get('s5_fc', 8)):
            u_, b_u = load_u(fc, fc % 2)
            for gpl in range(4):
                gp = fc * 4 + gpl
                for d in range(2):
                    gd = d * 32 + gp
                    gen_pw(gd, d)
                    self.cp('dve', pt('a64r')[:, gd:gd + 1], pw['r'][0][:, 64:65], [pw['r'][1]], [pb('a64r')])
                    self.cp('dve', pt('a64i')[:, gd:gd + 1], pw['i'][0][:, 64:65], [pw['i'][1]], [pb('a64i')])
                    k0 = 137 if d == 0 else 0
                    ctab(KS['r'][0], KS['r'][1], KS['i'][0], KS['i'][1], k0, 64, bb['r'][0], bb['i'][0], [bb['r'][1], bb['i'][1]], False)
                    for x in ('r', 'i'):
                        ptk = self.bank(1, 1024, BF16).rearrange('p (a b) -> p a b', a=8)
                        ksv = KS[x][0].rearrange('p (J j) c -> p J (j c)', J=8)
                        for J in range(8):
                            self.tr(ptk[:, J, :], ksv[:, J, :], self.ident, [KS[x][1], self.b_ident], [self.pbuf[1]])
                        self.cp('act', KST[x][0], ptk, [self.pbuf[1]], [KST[x][1]])
                    for gi in range(2):
                        gl = 2 * gpl + gi
                        rows = slice(gi * 64, (gi + 1) * 64)
                        for xi, x in enumerate(('r', 'i')):
                            pbk = 2 + xi
                            ps_ = self.bank(pbk, NSL)
                            for J in range(8):
                                self.mm(ps_, KST[x][0][:, J, :], u_[:, gl, J * NSL:(J + 1) * NSL], J == 0, J == 7,
                                        [KST[x][1], b_u], [self.pbuf[pbk]])
                            if d == 0:
                                self.cp('act', ST[x][0][rows, gd, 0:NSL], ps_[rows, 0:NSL], [self.pbuf[pbk]], [ST[x][1]])
                            else:
                                self.cp('act', ST[x][0][rows, gd, 0:NSL - 1], ps_[rows, 1:NSL], [self.pbuf[pbk]], [ST[x][1]])
        for d in range(2):
            colx = 20 if d == 0 else NSL - 1
            for x, src_ in (('r', self.s5re_in), ('i', self.s5im_in)):
                P.dma('sp', ST[x][0][:, d * 32:(d + 1) * 32, colx], src_[d].rearrange('(gp gi) p -> (gi p) gp', gi=2), [], [ST[x][1]],
                      ST[x][1], **kw)
        for d, eng in ((0, 'dve'), (1, 'pool')):
            sl = slice(d * 32, (d + 1) * 32)
            ar, ai = pt('a64r')[:, sl], pt('a64i')[:, sl]
            sc = {n: A.alloc([32, 5], F32, 's5sc%d%s' % (d, n)) for n in ('a', 'b')}
            xr, xi_ = ST['r'][0][:, sl, :], ST['i'][0][:, sl, :]
            for step in range(64):
                if d == 0:
                    m = step
                    if m < 4:
                        cS, pS, ns = slice(m + 1, m + 22, 5), slice(m, m + 21, 5), 5
                    else:
                        cS, pS, ns = slice(21 + m, 22 + m), slice(20 + m, 21 + m), 1
                else:
                    m = 63 - step
                    if m < 4:
                        cS, pS, ns = slice(m, m + 21, 5), slice(m + 1, m + 22, 5), 5
                    else:
                        cS, pS, ns = slice(20 + m, 21 + m), slice(21 + m, 22 + m), 1
                a_, b_a = sc['a'][0][:, :, 0:ns], sc['a'][1]
                b__, b_b = sc['b'][0][:, :, 0:ns], sc['b'][1]
                arb = ar.unsqueeze(2).to_broadcast([128, 32, ns])
                aib = ai.unsqueeze(2).to_broadcast([128, 32, ns])
                rdp = [ST['r'][1], ST['i'][1], pb('a64r'), pb('a64i')]
                self.tt(eng, a_, xr[:, :, pS], arb, ALU.mult, rdp, [b_a])
                self.tt(eng, b__, xi_[:, :, pS], aib, ALU.mult, rdp, [b_b])
                self.tt(eng, a_, a_, b__, ALU.subtract, [b_a, b_b], [b_a])
                self.tt(eng, b__, xi_[:, :, pS], arb, ALU.mult, rdp, [b_b])
                self.tt(eng, xr[:, :, cS], xr[:, :, cS], a_, ALU.add, [b_a, ST['r'][1]], [ST['r'][1]])
                self.tt(eng, a_, xr[:, :, pS], aib, ALU.mult, rdp, [b_a])
                self.tt(eng, a_, a_, b__, ALU.add, [b_a, b_b], [b_a])
                self.tt(eng, xi_[:, :, cS], xi_[:, :, cS], a_, ALU.add, [b_a, ST['i'][1]], [ST['i'][1]])
        for si in range(4):
            for d in range(2):
                colx = 5 * si + 4 if d == 0 else 5 * si
                for x, dst_ in (('r', self.s2re_out), ('i', self.s2im_out)):
                    P.dma('sp', dst_[si, d].rearrange('(gp gi) p -> (gi p) gp', gi=2), ST[x][0][:, d * 32:(d + 1) * 32, colx],
                          [ST[x][1]], [self.dbuf(('s2', x))], ST[x][1], **kw)
        mfb, b_mfb = A.alloc([2, 128], BF16, 's5mfb')
        self.cp('dve', mfb, mfb32, [b_mfb32], [b_mfb])
        KB = {(d, x): A.alloc([8, 16], BF16, 's5KB%d%s' % (d, x)) for d in range(2) for x in ('r', 'i')}
        QC = {(d, x): A.alloc([64, 16], BF16, 's5QC%d%s' % (d, x)) for d in range(2) for x in ('r', 'i')}
        QO = {(d, x): A.alloc([64, 16], BF16, 's5QO%d%s' % (d, x)) for d in range(2) for x in ('r', 'i')}
        TB = {(d, gi): A.alloc([8, 128], BF16, 's5TB%d%d' % (d, gi)) for d in range(2) for gi in range(2)}
        XB = {(d, x): A.alloc([NSL], BF16, 's5XB%d%s' % (d, x)) for d in range(2) for x in ('r', 'i')}
        yu, b_yu = A.alloc([8, 8 * NSL], BF16, 's5YU')
        for fc in range(DBG.get('s5_fc', 8)):
            u_, b_u = load_u(fc, fc % 2)
            for gpl in range(4):
                gp = fc * 4 + gpl
                for d in range(2):
                    gd = d * 32 + gp
                    gen_pw(gd, d)
                    cr, ci = Ct['r'][0][:, gd, :], Ct['i'][0][:, gd, :]
                    cb = [Ct['r'][1], Ct['i'][1]]
                    bbufs = [bb['r'][1], bb['i'][1]]
                    if d == 0:
                        ctab(KB[d, 'r'][0], KB[d, 'r'][1], KB[d, 'i'][0], KB[d, 'i'][1], 65, 8, bb['r'][0], bb['i'][0], bbufs, False)
                        ctab(QC[d, 'r'][0], QC[d, 'r'][1], QC[d, 'i'][0], QC[d, 'i'][1], 0, 64, cr, ci, cb, True)
                        ctab(QO[d, 'r'][0], QO[d, 'r'][1], QO[d, 'i'][0], QO[d, 'i'][1], 1, 64, cr, ci, cb, True)
                    else:
                        ctab(KB[d, 'r'][0], KB[d, 'r'][1], KB[d, 'i'][0], KB[d, 'i'][1], 0, 8, bb['r'][0], bb['i'][0], bbufs, False)
                        ctab(QC[d, 'r'][0], QC[d, 'r'][1], QC[d, 'i'][0], QC[d, 'i'][1], 73, 64, cr, ci, cb, True)
                        ctab(QO[d, 'r'][0], QO[d, 'r'][1], QO[d, 'i'][0], QO[d, 'i'][1], 137, 64, cr, ci, cb, True)
                    for x in ('r', 'i'):
                        self.cp('act', XB[d, x][0], ST[x][0][:, gd, :], [ST[x][1]], [XB[d, x][1]])
                    for gi in range(2):
                        rows = slice(gi * 64, (gi + 1) * 64)
                        tb_, b_tb = TB[d, gi]
                        for hf in range(2):
                            pbk = 2 + hf
                            pq = self.bank(pbk)
                            kbr = KB[d, 'r'][0].rearrange('p j c -> p (j c)')
                            kbi = KB[d, 'i'][0].rearrange('p j c -> p (j c)')
                            qcr = QC[d, 'r'][0].rearrange('p m c -> p (m c)')
                            qci = QC[d, 'i'][0].rearrange('p m c -> p (m c)')
                            self.mm(pq, kbr[rows, :], qcr[rows, hf * 512:(hf + 1) * 512], True, False,
                                    [KB[d, 'r'][1], QC[d, 'r'][1]], [self.pbuf[pbk]])
                            self.mm(pq, kbi[rows, :], qci[rows, hf * 512:(hf + 1) * 512], False, True,
                                    [KB[d, 'i'][1], QC[d, 'i'][1]], [self.pbuf[pbk]])
                            if hf == 0:
                                self.tt('dve', tb_[:, 0, :], pq[:, 0:128], mfb[:, d, :], ALU.mult, [self.pbuf[pbk], b_mfb], [b_tb])
                                self.cp('act', tb_[:, 1:4, :], pq[:, 128:512].rearrange('p (a b) -> p a b', a=3), [self.pbuf[pbk]], [b_tb])
                            else:
                                self.cp('act', tb_[:, 4:8, :], pq.rearrange('p (a b) -> p a b', a=4), [self.pbuf[pbk]], [b_tb])
                        if d == 0:
                            g = 2 * gp + gi
                            self.stt('dve', tb_[:, 0, :], identb, dU[:, g:g + 1], tb_[:, 0, :], ALU.mult, ALU.add,
                                     [self.b_ident, b_dU, b_tb], [b_tb])
                for gi in range(2):
                    gl = 2 * gpl + gi
                    rows = slice(gi * 64, (gi + 1) * 64)
                    py = self.psum[:, (4 + 2 * gi) * 512:(6 + 2 * gi) * 512]
                    bpy = [self.pbuf[4 + 2 * gi], self.pbuf[5 + 2 * gi]]
                    for I in range(8):
                        bk = I // 4
                        yo = py[:, bk * 512 + (I % 4) * NSL: bk * 512 + (I % 4 + 1) * NSL]
                        ops = []
                        for J in range(0, I + 1):
                            ops.append((yo, TB[0, gi][0][:, I - J, :], u_[:, gl, J * NSL:(J + 1) * NSL], [TB[0, gi][1], b_u]))
                        for J in range(I, 8):
                            ops.append((yo, TB[1, gi][0][:, J - I, :], u_[:, gl, J * NSL:(J + 1) * NSL], [TB[1, gi][1], b_u]))
                        for x in ('r', 'i'):
                            qo = QO[0, x][0].rearrange('p m c -> p (m c)')
                            ops.append((yo[:, 1:NSL], qo[rows, I * 128:(I + 1) * 128], XB[0, x][0][rows, 0:NSL - 1], [QO[0, x][1], XB[0, x][1]]))
                            qo = QO[1, x][0].rearrange('p m c -> p (m c)')
                            ops.append((yo, qo[rows, I * 128:(I + 1) * 128], XB[1, x][0][rows, 0:NSL], [QO[1, x][1], XB[1, x][1]]))
                        for oi, (o_, l_, r_, rd_) in enumerate(ops):
                            self.mm(o_, l_, r_, oi == 0, oi == len(ops) - 1, rd_, [bpy[bk]])
                    yv = yu[:, gl, :].rearrange('p (I n) -> p I n', I=8)
                    for bk in range(2):
                        self.act(yv[:, bk * 4:(bk + 1) * 4, :], py[:, bk * 512:bk * 512 + 4 * NSL].rearrange('p (I n) -> p I n', I=4),
                                 AF.Gelu_apprx_tanh, [bpy[bk]], [b_yu])
            for il in range(8):
                P.dma('sp', self.GTD[fc, :, il, :].rearrange('(gl c) x -> c gl x', c=16), yu[il * 16:(il + 1) * 16, :, :], [b_yu],
                      [self.dbuf(('GTD', fc))], b_yu, group=('yu', fc))
        P.barrier()
        A.push()
        gin = [A.alloc([8, 8, NSL], BF16, 's5gin%d' % k) for k in range(2)]
        otl = [A.alloc([TOK], BF16, 's5otl%d' % k) for k in range(2)]
        for fc in range(DBG.get('s5_fc', 8)):
            g_, b_g = gin[fc % 2]
            o_, b_o = otl[fc % 2]
            P.dma('sp', g_.rearrange('p i I n -> p i (I n)'), self.GTD[fc], [self.dbuf(('GTD', fc))], [b_g], b_g)
            for (s0, ln) in SEQS:
                n0, nch = s0 // 64, ln // 64
                c0 = self.s5_slot(n0)
                self.cp('dve' if fc % 2 == 0 else 'pool', o_[:, s0:s0 + ln].rearrange('p (n I i) -> p i I n', I=8, i=8),
                        g_[:, :, :, c0:c0 + nch], [b_g], [b_o])
            P.dma('sp', self.OTD[fc], o_, [b_o], [self.dbuf(('OTD', ti)) for ti in range(TOK // 128)], b_o)
        P.barrier()
        A.pop()


def make_consts():
    ident = np.eye(128, dtype=np.float32)
    masks = np.zeros((8, 128, 128), np.float32)
    for c in range(2):
        for blk in range(2):
            b0 = c * 64 + blk * 32
            for j in range(16):
                masks[0, b0 + 16 + j, b0 + j] = -1.0
                masks[0, b0 + j, b0 + 16 + j] = 1.0
    a = np.arange(128)
    same = (a[:, None] // 64) == (a[None, :] // 64)
    masks[1] = (same & (a[:, None] <= a[None, :])).astype(np.float32)
    masks[2] = (same & (a[:, None] >= a[None, :])).astype(np.float32)
    masks[3] = same.astype(np.float32)
    masks[4] = np.where(same & (a[None, :] < a[:, None]), 0.0, -30000.0)
    masks[5] = masks[4].T.copy()
    masks[6, :, 0] = (a < 64)
    masks[6, :, 1] = (a >= 64)
    lvlm = np.zeros((12, 128, 128), np.float32)
    for k in range(6):
        s = 1 << k
        mk = ((a[:, None] // (2 * s)) == (a[None, :] // (2 * s))) & ((a[:, None] % (2 * s)) < s) & ((a[None, :] % (2 * s)) >= s)
        lvlm[k] = mk
        lvlm[6 + k] = mk.T
    kvf = np.zeros(201, np.float32)
    kvb = np.zeros(201, np.float32)
    kvf[0:65] = np.arange(65)
    kvb[0:65] = np.arange(65)
    kvf[65:73] = -np.arange(8)
    m_ = np.arange(64)
    kvb[73:137] = 8 * (m_ // 8) - (m_ % 8)
    kvf[137:201] = 63 - m_
    kvb[137:201] = 64 - m_
    s5kv = np.stack([np.tile(kvf, (128, 1)), np.tile(kvb, (128, 1))]).astype(np.float32)
    jl = a // 16
    s5m = np.stack([(jl[None, :] >= jl[:, None]), (jl[:, None] >= jl[None, :])]).astype(np.float32)
    t = np.arange(NSAMP)
    rows = (t // 64).astype(np.float32)
    cols = (t % 64).astype(np.float32)
    inv = (np.float32(10000.0) ** (-np.arange(16, dtype=np.float32) / np.float32(16))).astype(np.float32)
    ang_r = rows[None, :] * inv[:, None]
    ang_c = cols[None, :] * inv[:, None]
    ang = np.concatenate([ang_r, ang_r, ang_c, ang_c], axis=0).astype(np.float32)
    ang = np.concatenate([ang, ang], axis=0)
    return dict(c_ident=ident, c_masks=masks, c_lvl=lvlm, c_s5kv=s5kv, c_s5m=s5m, c_cos=np.cos(ang).astype(np.float32), c_sin=np.sin(ang).astype(np.float32))


_CACHE = {}
DBG = {}
ATTACH_WAIT = True


def run(inputs, parts=None, x_override=None):
    key = repr(parts)
    if key not in _CACHE:
        _CACHE[key] = K(parts).build()
    nc = _CACHE[key]
    consts = make_consts()
    f = lambda a: np.ascontiguousarray(np.asarray(a, dtype=np.float32))
    in_maps = []
    for c in range(NCORES):
        m = dict(consts)
        xp = inputs['x_prompt'][4 * c:4 * c + 4].reshape(NPROMPT, D)
        xs = inputs['x_sample'][c]
        if x_override is not None:
            xp, xs = x_override[c]
        m['xp'] = f(xp)
        m['xs'] = f(xs)
        m['cond'] = f(np.stack([inputs['c_ctx'], inputs['c'][c]]))
        m['st0_in'] = f(inputs['state_l0_gdn'][c])
        m['ck_in'] = f(inputs['cache_l1_k'][c].reshape(256, D))
        m['cv_in'] = f(inputs['cache_l1_v'][c].reshape(256, D))
        m['s5re_in'] = f(inputs['state_l2_s5_re'][c])
        m['s5im_in'] = f(inputs['state_l2_s5_im'][c])
        m['st3_in'] = f(inputs['state_l3_gdn'][c])
        for n in WEIGHT_NAMES:
            m[n] = f(inputs[n])
        in_maps.append(m)
    ncr = DBG.get('cores', NCORES)
    res = run_bass_kernel_spmd(nc, in_maps[:ncr], core_ids=list(range(ncr)))
    return res.results


def kernel(**inputs):
    r = run(inputs)
    cat = lambda n: np.concatenate([np.asarray(r[c][n]) for c in range(NCORES)], axis=0)
    y_prompt = cat('yp').reshape(32, 256, D)
    y_sample = np.stack([np.asarray(r[c]['ys']) for c in range(NCORES)], axis=0)
    st0 = cat('st0_out')
    k1 = cat('k1_out').reshape(32, 256, 8, 2, 64)
    v1 = cat('v1_out').reshape(32, 256, 8, 128)
    s2re = cat('s2re_out')
    s2im = cat('s2im_out')
    st3 = cat('st3_out')
    return tuple(np.ascontiguousarray(a, dtype=np.float32) for a in (y_prompt, y_sample, st0, k1, v1, s2re, s2im, st3))
```
